# Optimizing a Trainium2 kernel written in Bass

```python
import math
import jax, jax.numpy as jnp
from jax import lax
import numpy as np

D_MODEL = 1024
BATCH = 8
SEQ = 2048
DEPTH = 1
DEC_BATCH = 2
DEC_SEQ = 8192
PAST_LEN = 128

GRID_W = 64
MIX_WIDTH = D_MODEL
ATT_WIDTH = MIX_WIDTH // 2
HEAD_DIM = 64
N_HEADS = ATT_WIDTH // HEAD_DIM
POOL_WIDTH = MIX_WIDTH - ATT_WIDTH
POOL_WINDOWS = (2, 4, 8, 16)
N_POOL = len(POOL_WINDOWS)
POOL_GROUP_DIM = POOL_WIDTH // N_POOL
IN_WIDTH = 3 * ATT_WIDTH + POOL_WIDTH
WIN_ROWS = 8
WIN_COLS = 16
D_FF = 2816
CONV_W = 3
EPS = 1e-6

kernel_name = "hybrid_natten_pool_encoder"


def rmsnorm(x, g):
    xf = x.astype(jnp.float32)
    y = xf * lax.rsqrt(jnp.mean(xf * xf, axis=-1, keepdims=True) + EPS)
    return (y * g.astype(jnp.float32)).astype(x.dtype)


def neighborhood_attention(q, k, v, rpb):
    B, T, H, hd = q.shape
    rows = T // GRID_W
    kh = min(WIN_ROWS, rows)
    kw = WIN_COLS
    qg = q.reshape(B, rows, GRID_W, H, hd)
    kg = k.reshape(B, rows, GRID_W, H, hd)
    vg = v.reshape(B, rows, GRID_W, H, hd)
    cols = jnp.arange(GRID_W)
    col_start = jnp.clip(cols - kw // 2, 0, GRID_W - kw)
    col_idx = col_start[:, None] + jnp.arange(kw)[None, :]
    dc = col_idx - cols[:, None] + (WIN_COLS - 1)
    scale = HEAD_DIM ** -0.5

    def row_block(r):
        rs = jnp.clip(r - kh // 2, 0, rows - kh)
        q_r = lax.dynamic_index_in_dim(qg, r, axis=1, keepdims=False)
        k_rows = lax.dynamic_slice_in_dim(kg, rs, kh, axis=1)
        v_rows = lax.dynamic_slice_in_dim(vg, rs, kh, axis=1)
        k_nb = k_rows[:, :, col_idx]
        v_nb = v_rows[:, :, col_idx]
        s = jnp.einsum('bqhd,biqjhd->bhqij', q_r, k_nb)
        dr = rs + jnp.arange(kh) - r + (WIN_ROWS - 1)
        bias = rpb[:, dr[None, :, None], dc[:, None, :]]
        s = s.astype(jnp.float32) * scale + bias.astype(jnp.float32)[None]
        p = jax.nn.softmax(s.reshape(B, H, GRID_W, kh * kw), axis=-1)
        p = p.reshape(B, H, GRID_W, kh, kw).astype(v.dtype)
        return jnp.einsum('bhqij,biqjhd->bqhd', p, v_nb)

    out = lax.map(row_block, jnp.arange(rows))
    return jnp.transpose(out, (1, 0, 2, 3, 4)).reshape(B, T, H * hd)


def multiscale_pool(u, w_pool, pool_scale):
    B, T, _ = u.shape
    ug = u.reshape(B, T, N_POOL, POOL_GROUP_DIM).astype(jnp.float32)
    cs = jnp.concatenate([jnp.zeros((B, 1, N_POOL, POOL_GROUP_DIM), jnp.float32),
                          jnp.cumsum(ug, axis=1)], axis=1)
    t = jnp.arange(T)
    means = []
    for g, w in enumerate(POOL_WINDOWS):
        lo = jnp.clip(t - w // 2, 0, T)
        hi = jnp.clip(t - w // 2 + w, 0, T)
        cnt = (hi - lo).astype(jnp.float32)
        means.append((cs[:, hi, g] - cs[:, lo, g]) / cnt[None, :, None])
    mixed = (jnp.stack(means, axis=2) - ug).astype(u.dtype)
    y = jnp.einsum('btgc,gcd->btgd', mixed, w_pool)
    return y.reshape(B, T, POOL_WIDTH) * pool_scale


def dwconv3(x, w, b):
    xp = jnp.pad(x, ((0, 0), (1, 1), (0, 0)))
    return xp[:, :-2] * w[0] + xp[:, 1:-1] * w[1] + xp[:, 2:] * w[2] + b


def encoder_layer(x, c, w_ada, b_ada, norm1_g, norm2_g, w_in, q_norm_g, k_norm_g,
                  rpb, w_pool, pool_scale, w_out, w_up, conv_w, conv_b, w_down):
    B, T, D = x.shape
    mod = jnp.einsum('bd,de->be', jax.nn.silu(c), w_ada) + b_ada
    sh1, sc1, g1, sh2, sc2, g2 = jnp.split(mod[:, None, :], 6, axis=-1)

    h = rmsnorm(x, norm1_g) * (1 + sc1) + sh1
    proj = jnp.einsum('btd,de->bte', h, w_in)
    q, k, v, u = jnp.split(proj, [ATT_WIDTH, 2 * ATT_WIDTH, 3 * ATT_WIDTH], axis=-1)
    q = rmsnorm(q.reshape(B, T, N_HEADS, HEAD_DIM), q_norm_g)
    k = rmsnorm(k.reshape(B, T, N_HEADS, HEAD_DIM), k_norm_g)
    v = v.reshape(B, T, N_HEADS, HEAD_DIM)
    a = neighborhood_attention(q, k, v, rpb)
    p = multiscale_pool(u, w_pool, pool_scale)
    mixed = jnp.einsum('bte,ed->btd', jnp.concatenate([a, p], axis=-1), w_out)
    x = x + g1 * mixed

    h = rmsnorm(x, norm2_g) * (1 + sc2) + sh2
    up = dwconv3(jnp.einsum('btd,df->btf', h, w_up), conv_w, conv_b)
    gate, val = jnp.split(up, 2, axis=-1)
    y = jnp.einsum('btf,fd->btd', jax.nn.silu(gate) * val, w_down)
    return x + g2 * y


def setup_inputs(seed: int = 0) -> dict:
    key = jax.random.key(seed)
    ks = jax.random.split(key, 24)
    nrm = jax.random.normal
    L, D, F = DEPTH, D_MODEL, D_FF
    f32 = jnp.float32
    conv_center = jnp.zeros((CONV_W, 1), f32).at[CONV_W // 2].set(1.0)
    return {
        "x_prompt": nrm(ks[0], (BATCH, SEQ, D), f32),
        "x_sample": nrm(ks[1], (DEC_BATCH, DEC_SEQ, D), f32),
        "c_prompt": nrm(ks[2], (BATCH, D), f32),
        "c_sample": nrm(ks[3], (DEC_BATCH, D), f32),
        "w_ada": nrm(ks[4], (L, D, 6 * D), f32) * (0.5 * D ** -0.5),
        "b_ada": nrm(ks[5], (L, 6 * D), f32) * 0.02,
        "norm1_g": 1.0 + 0.05 * nrm(ks[6], (L, D), f32),
        "norm2_g": 1.0 + 0.05 * nrm(ks[7], (L, D), f32),
        "w_in": nrm(ks[8], (L, D, IN_WIDTH), f32) * D ** -0.5,
        "q_norm_g": 1.0 + 0.05 * nrm(ks[9], (L, HEAD_DIM), f32),
        "k_norm_g": 1.0 + 0.05 * nrm(ks[10], (L, HEAD_DIM), f32),
        "rpb": 0.5 * nrm(ks[11], (L, N_HEADS, 2 * WIN_ROWS - 1, 2 * WIN_COLS - 1), f32),
        "w_pool": nrm(ks[12], (L, N_POOL, POOL_GROUP_DIM, POOL_GROUP_DIM), f32) * POOL_GROUP_DIM ** -0.5,
        "pool_scale": 1.0 + 0.1 * nrm(ks[13], (L, POOL_WIDTH), f32),
        "w_out": nrm(ks[14], (L, MIX_WIDTH, D), f32) * MIX_WIDTH ** -0.5,
        "w_up": nrm(ks[15], (L, D, 2 * F), f32) * D ** -0.5,
        "conv_w": conv_center[None] + 0.2 * nrm(ks[16], (L, CONV_W, 2 * F), f32),
        "conv_b": 0.02 * nrm(ks[17], (L, 2 * F), f32),
        "w_down": nrm(ks[18], (L, F, D), f32) * F ** -0.5,
    }


def reference(x_prompt, x_sample, c_prompt, c_sample, w_ada, b_ada, norm1_g, norm2_g,
              w_in, q_norm_g, k_norm_g, rpb, w_pool, pool_scale, w_out, w_up,
              conv_w, conv_b, w_down):
    y_prompt = x_prompt
    y_sample = x_sample
    for l in range(DEPTH):
        params = (w_ada[l], b_ada[l], norm1_g[l], norm2_g[l], w_in[l], q_norm_g[l],
                  k_norm_g[l], rpb[l], w_pool[l], pool_scale[l], w_out[l], w_up[l],
                  conv_w[l], conv_b[l], w_down[l])
        y_prompt = encoder_layer(y_prompt, c_prompt, *params)
        y_sample = encoder_layer(y_sample, c_sample, *params)
    return (y_prompt, y_sample)
```

```python
import numpy as np
import ml_dtypes
from contextlib import ExitStack
import concourse.bass as bass
import concourse.mybir as mybir
from concourse.bass_utils import run_bass_kernel_spmd

F32 = mybir.dt.float32
BF16 = mybir.dt.bfloat16
AF = mybir.ActivationFunctionType
ALU = mybir.AluOpType

D = 1024
NKC = 8
NH = 8
HD = 64
FF = 2816
NJ = 22
GRID_W = 64
WIN_COLS = 16
EPS = 1e-6
NEG = -30000.0
POOL_WINDOWS = (2, 4, 8, 16)
N_CORES = 8


class PartGeom:
    pass


def make_geom(part):
    g = PartGeom()
    g.part = part
    if part == 0:
        g.ntile = 16
        g.nkv = 16
        g.row0 = 0
        g.qtiles = list(range(16))
        g.nslot = 16
        g.seq_rows = 32
    else:
        g.ntile = 22
        g.nkv = 21
        g.row0 = -6
        g.qtiles = list(range(3, 19)) + [21]
        g.nslot = 17
        g.seq_rows = 128
    blocks = []
    for m in range(8):
        if part == 0:
            qt0 = 2 * m
            pairs = [(2 * m - 2 + o, 11 - 2 * o, o) for o in range(6) if 0 <= 2 * m - 2 + o <= 15]
            slot0 = 2 * m
        else:
            qt0 = 3 + 2 * m
            pairs = [(2 * m + 1 + o, 11 - 2 * o, o) for o in range(6)]
            slot0 = 2 * m
        blocks.append(dict(qcol0=qt0 * 128, nrow=4, scol0=slot0 * 128, qrow0=4 * m, pairs=pairs))
    if part == 1:
        blocks.append(dict(qcol0=21 * 128, nrow=1, scol0=16 * 128, qrow0=-1,
                           pairs=[(o, 12 - 2 * o, o) for o in range(5)]))
        blocks.append(dict(qcol0=21 * 128 + 64, nrow=1, scol0=16 * 128 + 64, qrow0=32,
                           pairs=[(17 + o, 11 - 2 * o, o) for o in range(4)]))
    g.blocks = blocks
    return g


def key_row_of_tile(g, t):
    return g.row0 + 2 * t


def row_valid(seq_rows, R0, qr_local, kr_local):
    gq = R0 + qr_local
    gk = R0 + kr_local
    if gq < 0 or gq >= seq_rows:
        return True
    if gk < 0 or gk >= seq_rows:
        return False
    rs = min(max(gq - 4, 0), seq_rows - 8)
    return rs <= gk <= rs + 7


def part_R0(part, core):
    return 0 if part == 0 else 32 * (core % 4)


def mask_layout(g):
    idx = {}
    n = 0
    anyv = {}
    allv = {}
    for bi, blk in enumerate(g.blocks):
        for (kt, sp, o) in blk['pairs']:
            kr0 = key_row_of_tile(g, kt)
            for i in range(blk['nrow']):
                idx[(bi, o, i)] = n
                n += 1
                a = False
                al = True
                for core in range(N_CORES):
                    R0 = part_R0(g.part, core)
                    for half in range(2):
                        v = row_valid(g.seq_rows, R0, blk['qrow0'] + i, kr0 + half)
                        a = a or v
                        al = al and v
                anyv[(bi, o, i)] = a
                allv[(bi, o, i)] = al
    return idx, n, anyv, allv


GEOMS = [make_geom(0), make_geom(1)]
MASKS = [mask_layout(g) for g in GEOMS]


def pool_matrix(T0, TS):
    M = np.zeros((3, 4, 128, 128), np.float32)
    for gi, w in enumerate(POOL_WINDOWS):
        for t in range(128):
            tout = T0 + t
            if tout < 0 or tout >= TS:
                continue
            lo = min(max(tout - w // 2, 0), TS)
            hi = min(max(tout - w // 2 + w, 0), TS)
            cnt = hi - lo
            for tin in range(lo, hi):
                rel = tin - T0
                j = rel // 128 + 1
                M[j, gi, rel % 128, t] += 1.0 / cnt
            M[1, gi, t, t] -= 1.0
    return M


def host_tables(core):
    ch = core % 4
    R0s = 32 * ch
    tabs = {}
    mp = np.stack([
        pool_matrix(0, 2048),
        pool_matrix(4096, 8192 * 4),
        pool_matrix(2048 - 128, 2048),
        pool_matrix(R0s * 64, 8192),
        pool_matrix((R0s + 30) * 64, 8192),
    ])
    tabs['mpool'] = np.ascontiguousarray(mp.transpose(3, 0, 1, 2, 4).reshape(128, 60, 128)).astype(ml_dtypes.bfloat16)
    for p in range(2):
        g = GEOMS[p]
        idx, n, _, _ = MASKS[p]
        R0 = part_R0(p, core)
        rm = np.zeros((128, n), np.float32)
        for bi, blk in enumerate(g.blocks):
            for (kt, sp, o) in blk['pairs']:
                kr0 = key_row_of_tile(g, kt)
                for i in range(blk['nrow']):
                    for half in range(2):
                        if not row_valid(g.seq_rows, R0, blk['qrow0'] + i, kr0 + half):
                            rm[half * 64:(half + 1) * 64, idx[(bi, o, i)]] = NEG
        tabs['rmask%d' % p] = rm
    hv = np.zeros((128, 2), np.float32)
    hv[:, 0] = 1.0 if ch > 0 else 0.0
    hv[:, 1] = 1.0 if ch < 3 else 0.0
    tabs['halov'] = hv
    return tabs


def static_tables():
    cols = np.arange(GRID_W)
    cs = np.clip(cols - WIN_COLS // 2, 0, GRID_W - WIN_COLS)
    cm = np.zeros((128, 64), np.float32)
    for kc in range(64):
        for qc in range(64):
            ok = cs[qc] <= kc < cs[qc] + WIN_COLS
            if not ok:
                cm[kc, qc] = NEG
                cm[64 + kc, qc] = NEG
    ident = np.eye(128, dtype=np.float32)
    bones = np.zeros((128, 128), np.float32)
    bones[:64, :64] = 1
    bones[64:, 64:] = 1
    sel = np.zeros((2, 2, 128), np.float32)
    sel[0, 0, :] = 1
    sel[1, 1, :] = 1
    return dict(cmask=cm, ident=ident.astype(ml_dtypes.bfloat16), bones=bones.astype(ml_dtypes.bfloat16),
                sel=sel.reshape(2, 256), i2=np.eye(2, dtype=np.float32))


def rpb_gather_index():
    dr = np.zeros((128, 16, 64), np.int64)
    dc = np.zeros((128, 16, 64), np.int64)
    for p in range(128):
        half, kc = p // 64, p % 64
        for s in range(16):
            delta = 7 - s
            for qc in range(64):
                dr[p, s, qc] = min(max(delta + half + 7, 0), 14)
                dc[p, s, qc] = min(max(kc - qc + 15, 0), 30)
    return dr, dc


class Trk:
    def __init__(self, nc, stack):
        self.nc = nc
        self.stack = stack
        self.engs = {'pe': nc.tensor, 'act': nc.scalar, 'dve': nc.vector, 'pool': nc.gpsimd, 'sp': nc.sync}
        self.esem = {k: stack.enter_context(nc.semaphore('e_' + k)) for k in ('pe', 'act', 'dve', 'pool')}
        self.ecnt = {k: 0 for k in self.esem}
        self.waited = {}
        self.res = {}
        self.dsem = {}
        self.dcnt = {}
        self.nops = 0

    def _sem(self, tok):
        kind, key, val = tok
        return self.esem[key] if kind == 'e' else self.dsem[key]

    def _wait(self, eng, tok):
        if tok is None:
            return
        kind, key, val = tok
        if kind == 'e' and key == 'pe' and eng == 'pe':
            return
        wk = (eng, kind, key)
        if self.waited.get(wk, 0) >= val:
            return
        self.engs[eng].wait_ge(self._sem(tok), val)
        self.waited[wk] = val

    def deps(self, eng, reads, writes):
        for r in reads:
            e = self.res.get(r)
            if e:
                self._wait(eng, e[0])
        for w in writes:
            e = self.res.get(w)
            if e:
                self._wait(eng, e[0])
                for t in e[1].values():
                    self._wait(eng, t)

    def commit(self, tok, reads, writes):
        for r in reads:
            e = self.res.setdefault(r, [None, {}])
            k = (tok[0], tok[1])
            if k not in e[1] or e[1][k][2] < tok[2]:
                e[1][k] = tok
        for w in writes:
            self.res[w] = [tok, {}]

    def op(self, eng, fn, reads=(), writes=()):
        self.deps(eng, reads, writes)
        ins = fn()
        self.ecnt[eng] += 1
        ins.then_inc(self.esem[eng], 1)
        tok = ('e', eng, self.ecnt[eng])
        self.commit(tok, reads, writes)
        self.nops += 1

    def dma(self, eng, key, fns, reads=(), writes=()):
        if key not in self.dsem:
            self.dsem[key] = self.stack.enter_context(self.nc.semaphore('d_' + key))
            self.dcnt[key] = 0
        if self.dcnt[key] > 0:
            self._wait(eng, ('d', key, self.dcnt[key]))
        self.deps(eng, reads, writes)
        for fn in fns:
            ins = fn()
            ins.then_inc(self.dsem[key], 16)
            self.dcnt[key] += 16
        tok = ('d', key, self.dcnt[key])
        self.commit(tok, reads, writes)
        return tok

    def barrier(self):
        toks = [('e', k, self.ecnt[k]) for k in self.esem if self.ecnt[k] > 0]
        toks += [('d', k, self.dcnt[k]) for k in self.dsem if self.dcnt[k] > 0]
        for eng in ('pe', 'act', 'dve', 'pool', 'sp'):
            for t in toks:
                self._wait(eng, t)


def build_nc():
    nc = bass.Bass("TRN2", target_bir_lowering=False)

    def din(name, shape, dt=F32):
        return nc.dram_tensor(name, list(shape), dt, kind="ExternalInput").ap()

    xin = [din("xp", [2048, D]), din("xs", [22 * 128, D])]
    yout = [nc.dram_tensor("yp", [2048, D], F32, kind="ExternalOutput").ap(),
            nc.dram_tensor("ys", [2048, D], F32, kind="ExternalOutput").ap()]
    cT_d = din("cT", [128, 16])
    wada_d = din("w_ada", [D, 6 * D])
    bada_d = din("b_ada2", [2, 6 * D])
    ng_d = din("ng", [128, 16])
    win_d = din("w_in", [D, 2048])
    qkg_d = din("qkg", [128, 2])
    rpbG_d = din("rpbG", [NH, 128, 1024])
    cmask_d = din("cmask", [128, 64])
    wpool_d = din("w_pool", [4, 128, 128])
    pscale_d = din("pscale", [128, 4])
    wout_d = din("w_out", [D, D])
    wup_d = din("w_up", [D, 2 * FF])
    convw_d = din("convw", [128, 44 * 3])
    convb_d = din("convb", [128, 44])
    wdn_d = din("w_down", [FF, D])
    mpool_d = din("mpool", [128, 60, 128], BF16)
    rmask_d = [din("rmask0", [128, MASKS[0][1]]), din("rmask1", [128, MASKS[1][1]])]
    halov_d = din("halov", [128, 2])
    ident_d = din("ident", [128, 128], BF16)
    bones_d = din("bones", [128, 128], BF16)
    sel_d = din("sel", [2, 256])
    i2_d = din("i2", [2, 2])

    wada_v = wada_d.rearrange("(kc p) e -> p kc e", p=128)
    win_v = win_d.rearrange("(kc p) e -> p kc e", p=128)
    wout_v = wout_d.rearrange("(kc p) e -> p kc e", p=128)
    wup_v = wup_d.rearrange("(kc p) e -> p kc e", p=128)
    wdn_v = wdn_d.rearrange("(j p) e -> p j e", p=128)
    wpool_v = wpool_d.rearrange("g c d -> c g d")

    stack = ExitStack()
    with stack:
        T = Trk(nc, stack)

        def sb(name, shape, dt):
            return stack.enter_context(nc.sbuf_tensor("s_" + name, list(shape), dt))

        ident = sb("ident", [128, 128], BF16)
        bones = sb("bones", [128, 128], BF16)
        ones_bf = sb("ones_bf", [128, 64], BF16)
        sel = sb("sel", [2, 256], F32)
        i2 = sb("i2", [2, 2], F32)
        cT = sb("cT", [128, 16], F32)
        scT = sb("scT", [128, 16], BF16)
        ng = sb("ng", [128, 16], F32)
        qkg = sb("qkg", [128, 2], F32)
        pscale = sb("pscale", [128, 4], F32)
        convw = sb("convw", [128, 44 * 3], F32)
        convb = sb("convb", [128, 44], F32)
        halov = sb("halov", [128, 2], F32)
        cmask = sb("cmask", [128, 64], F32)
        modF = sb("modF", [128, 64], F32)
        amod = sb("amod", [128, 32], F32)
        gate = sb("gate", [128, 4, 1024], F32)
        rmask = [sb("rmask0", [128, MASKS[0][1]], F32), sb("rmask1", [128, MASKS[1][1]], F32)]
        stat = sb("stat", [128, 64], F32)
        AR_ELEMS = 95488
        arena = sb("arena", [128, AR_ELEMS], BF16)
        ps = stack.enter_context(nc.psum_tensor("ps", [128, 4096], F32))

        KB = 512

        def abf(off_kb, shape):
            n = int(np.prod(shape))
            o = int(off_kb * KB)
            v = arena[:, o:o + n]
            if len(shape) == 2:
                v = v.rearrange("p (a b) -> p a b", a=shape[0])
            elif len(shape) == 3:
                v = v.rearrange("p (a b c) -> p a b c", a=shape[0], b=shape[1])
            return v

        def af32(off_kb, shape):
            n = int(np.prod(shape))
            o = int(off_kb * KB)
            v = arena[:, o:o + 2 * n].bitcast(F32)
            if len(shape) == 2:
                v = v.rearrange("p (a b) -> p a b", a=shape[0])
            elif len(shape) == 3:
                v = v.rearrange("p (a b c) -> p a b c", a=shape[0], b=shape[1])
            return v

        def bank(i, n=512, dt=F32):
            v = ps[:, i * 512:i * 512 + n]
            return v

        def ld(dst, src, key, eng='sp'):
            T.dma(eng, key, [lambda: T.engs[eng].dma_start(out=dst, in_=src)], writes=[key])

        ld(ident[:], ident_d, 'ident')
        ld(bones[:], bones_d, 'bones')
        ld(sel[:], sel_d, 'sel')
        ld(i2[:], i2_d, 'i2')
        ld(cT[:], cT_d, 'cT')
        ld(ng[:], ng_d, 'ng')
        ld(qkg[:], qkg_d, 'qkg')
        ld(pscale[:], pscale_d, 'pscale')
        ld(convw[:], convw_d, 'convw')
        ld(convb[:], convb_d, 'convb')
        ld(halov[:], halov_d, 'halov')
        ld(cmask[:], cmask_d, 'cmask')
        ld(rmask[0][:], rmask_d[0], 'rmask0')
        ld(rmask[1][:], rmask_d[1], 'rmask1')
        T.op('dve', lambda: nc.vector.memset(ones_bf[:], 1.0), writes=['ones_bf'])
        T.op('dve', lambda: nc.vector.tensor_scalar(out=qkg[:, 0:1], in0=qkg[:, 0:1], scalar1=0.125, scalar2=None,
                                                    op0=ALU.mult), reads=['qkg'], writes=['qkg'])

        T.op('act', lambda: nc.scalar.activation(out=scT[:], in_=cT[:], func=AF.Silu), reads=['cT'], writes=['scT'])
        wada_sb = [abf(0, [8, 1024]), abf(16, [8, 1024])]
        modrow = af32(32, [1024])
        brow = af32(36, [1024])
        psF = bank(6, 64)
        fm_blocks = {0: 0, 1: 1, 3: 2, 4: 3}
        for blk in range(6):
            wb = wada_sb[blk % 2]
            wkey = 'wada%d' % (blk % 2)
            T.dma('pool', wkey, [
                (lambda kc=kc: nc.gpsimd.dma_start(out=wb[:, kc, :], in_=wada_v[:, kc, blk * 1024:(blk + 1) * 1024]))
                for kc in range(8)], writes=[wkey])
            T.dma('sp', 'brow', [lambda: nc.sync.dma_start(out=brow[0:2, :], in_=bada_d[:, blk * 1024:(blk + 1) * 1024])],
                  writes=['brow'])

            def mm_ada():
                for half in range(2):
                    for kc in range(8):
                        ins = nc.tensor.matmul(bank(half)[0:2, :], lhsT=scT[:, 2 * kc:2 * kc + 2],
                                               rhs=wb[:, kc, half * 512:(half + 1) * 512],
                                               start=(kc == 0), stop=(kc == 7))
                return ins
            T.op('pe', mm_ada, reads=['scT', wkey], writes=['bank0', 'bank1'])
            T.op('dve', lambda: nc.vector.tensor_tensor(out=modrow[0:2, :], in0=ps[0:2, 0:1024], in1=brow[0:2, :],
                                                        op=ALU.add),
                 reads=['bank0', 'bank1', 'brow'], writes=['modrow'])
            if blk in fm_blocks:
                fb = fm_blocks[blk]

                def mm_fm():
                    for kc in range(8):
                        ins = nc.tensor.matmul(psF[:, (fb * 8 + kc) * 2:(fb * 8 + kc) * 2 + 2],
                                               lhsT=modrow[0:2, kc * 128:(kc + 1) * 128], rhs=i2[0:2, :],
                                               start=True, stop=True)
                    return ins
                T.op('pe', mm_fm, reads=['modrow', 'i2'], writes=['bank6'])
            else:
                gi = 0 if blk == 2 else 1
                for b in range(2):
                    def mm_g():
                        for half in range(2):
                            ins = nc.tensor.matmul(bank(2 + half), lhsT=sel[0:2, b * 128:(b + 1) * 128],
                                                   rhs=modrow[0:2, half * 512:(half + 1) * 512],
                                                   start=True, stop=True)
                        return ins
                    T.op('pe', mm_g, reads=['modrow', 'sel'], writes=['bank2', 'bank3'])
                    T.op('act', lambda: nc.scalar.copy(out=gate[:, b * 2 + gi, :], in_=ps[:, 1024:2048]),
                         reads=['bank2', 'bank3'], writes=['gate'])
        T.op('dve', lambda: nc.vector.tensor_copy(out=modF[:], in_=psF), reads=['bank6'], writes=['modF'])
        for which in range(2):
            scblk = 1 if which == 0 else 3
            T.op('dve', lambda: nc.vector.tensor_scalar(out=amod[:, which * 16:(which + 1) * 16],
                                                        in0=modF[:, scblk * 16:(scblk + 1) * 16],
                                                        scalar1=1.0, scalar2=None, op0=ALU.add),
                 reads=['modF'], writes=['amod'])
            ngv = ng[:, which * 8:(which + 1) * 8].unsqueeze(2).to_broadcast([128, 8, 2])
            av = amod[:, which * 16:(which + 1) * 16].rearrange("p (k b) -> p k b", b=2)
            T.op('dve', lambda: nc.vector.tensor_tensor(out=av, in0=av, in1=ngv, op=ALU.mult),
                 reads=['amod', 'ng'], writes=['amod'])
        T.barrier()

        store_toks = []
        for part in range(2):
            g = GEOMS[part]
            midx, mn, many, mall = MASKS[part]
            b = part
            NT = g.ntile
            TC = NT * 128
            NS = g.nslot
            SC = NS * 128
            hT = abf(0, [8, 2816])
            win = abf(44, [8, 2048])
            wpool = abf(76, [4, 128])
            uv = abf(77, [22, 512])
            catT = abf(99, [8, 2176])
            wout = abf(169.5, [8, 1024])
            vaug = abf(169.5, [22, 2, 128])
            xt = [af32(133 + 4 * i, [1024]) for i in range(3)]
            xn = [abf(145 + 2 * i, [1024]) for i in range(4)]
            hmul = [af32(155 + 4 * i, [8, 128]) for i in range(4)]
            junk = [abf(153, [1024]), abf(171, [1024])]
            mpool = abf(133, [60, 128])
            mixT = [abf(148 + i, [4, 128]) for i in range(2)]
            qc = abf(133, [2816])
            kc_ = abf(138.5, [2816])
            T2 = af32(144, [2, 1024])
            T3m = af32(152, [2, 10, 64])
            PT = [abf(157 + 0.5 * i, [256]) for i in range(6)]
            rd = [af32(161.5 + i, [256]) for i in range(2)]
            sq = [abf(163.5 + i, [512]) for i in range(2)]
            rs = [af32(165.5 + 2 * i, [512]) for i in range(2)]
            x1 = af32(0, [17, 1024])
            otmp = [af32(133 + 4 * i, [1024]) for i in range(2)]

            a1 = amod[:, 0:16].rearrange("p (k b) -> p k b", b=2)
            sh1 = modF[:, 0:16].rearrange("p (k b) -> p k b", b=2)
            a2 = amod[:, 16:32].rearrange("p (k b) -> p k b", b=2)
            sh2 = modF[:, 32:48].rearrange("p (k b) -> p k b", b=2)

            for kc in range(8):
                T.dma('pool', 'win%d' % kc,
                      [lambda kc=kc: nc.gpsimd.dma_start(out=win[:, kc, :], in_=win_v[:, kc, :])],
                      writes=['win%d' % kc])
            T.dma('pool', 'wpool', [lambda: nc.gpsimd.dma_start(out=wpool, in_=wpool_v)], writes=['wpool'])
            WIN_ALL = ['win%d' % kc for kc in range(8)]

            def norm_tile(src_ap, src_res, n, amodv, shv, dst_fn, dst_res, sidx):
                pass

            def rms_stage_a(src_ap, src_res, i, sidx):
                jb = junk[i % 2]
                sc = stat[:, sidx * 2:sidx * 2 + 1]
                sr = stat[:, sidx * 2 + 1:sidx * 2 + 2]
                sres = 'stat%d' % sidx
                T.op('act', lambda: nc.scalar.activation(out=jb, in_=src_ap, func=AF.Square, accum_out=sc),
                     reads=[src_res], writes=['junk%d' % (i % 2), sres])
                T.op('act', lambda: nc.scalar.activation(out=sr, in_=sc, func=AF.Sqrt, scale=1.0 / D, bias=epsb[:, 0:1]),
                     reads=[sres, 'epsb'], writes=[sres + 'r'])
                T.op('dve', lambda: nc.vector.reciprocal(out=sr, in_=sr), reads=[sres + 'r'], writes=[sres + 'r'])

            def rms_stage_b(src_ap, src_res, i, sidx):
                xnb = xn[i % len(xn)]
                xnres = 'xn%d' % (i % len(xn))
                sr = stat[:, sidx * 2 + 1:sidx * 2 + 2]
                sres = 'stat%d' % sidx
                T.op('act', lambda: nc.scalar.activation(out=xnb, in_=src_ap, func=AF.Copy, scale=sr),
                     reads=[src_res, sres + 'r'], writes=[xnres])
                bk = i % 2
                pT = ps[:, bk * 512:(bk + 1) * 512].bitcast(BF16).rearrange("p (k t) -> p k t", k=8)

                def tr():
                    for kc in range(8):
                        ins = nc.tensor.transpose(pT[:, kc, :], xnb[:, kc * 128:(kc + 1) * 128], ident[:])
                    return ins
                T.op('pe', tr, reads=[xnres, 'ident'], writes=['bank%d' % bk])
                return pT, 'bank%d' % bk

            def recip(out_ap, in_ap, scratch_ap, reads, writes, sres):
                T.op('dve', lambda: nc.vector.reciprocal_approx_accurate(out=out_ap, in_=in_ap, scratch=scratch_ap),
                     reads=reads, writes=list(writes) + [sres])

            epsb = sb("epsb%d" % part, [128, 2], F32)
            T.op('dve', lambda: nc.vector.memset(epsb[:, 0:1], EPS), writes=['epsb'])
            T.op('dve', lambda: nc.vector.memset(epsb[:, 1:2], EPS), writes=['epsb'])

            def a1_tail(t):
                pT, bres = rms_stage_b(xt[t % 3], 'xt%d' % (t % 3), t, t % 8)
                def mod1():
                    for kc in range(8):
                        ins = nc.vector.scalar_tensor_tensor(
                            out=hT[:, kc, t * 128:(t + 1) * 128], in0=pT[:, kc, :], scalar=a1[:, kc, b:b + 1],
                            in1=sh1[:, kc, b:b + 1].to_broadcast([128, 128]), op0=ALU.mult, op1=ALU.add)
                    return ins
                T.op('dve', mod1, reads=[bres, 'amod', 'modF'], writes=['hT%d' % t])

            def inproj_tok(t, col0, dst_res, evac='act'):
                bk = 2 + (t % 2)

                def mm():
                    for kc in range(8):
                        ins = nc.tensor.matmul(bank(bk), lhsT=hT[:, kc, t * 128:(t + 1) * 128],
                                               rhs=win[:, kc, col0:col0 + 512], start=(kc == 0), stop=(kc == 7))
                    return ins
                T.op('pe', mm, reads=['hT%d' % t] + WIN_ALL, writes=['bank%d' % bk])
                if evac == 'act':
                    T.op('act', lambda: nc.scalar.copy(out=uv[:, t, :], in_=bank(bk)), reads=['bank%d' % bk],
                         writes=[dst_res])
                else:
                    T.op('dve', lambda: nc.vector.tensor_copy(out=uv[:, t, :], in_=bank(bk)), reads=['bank%d' % bk],
                         writes=[dst_res])

            for t in range(NT + 1):
                if t < NT:
                    xb = xt[t % 3]
                    xres = 'xt%d' % (t % 3)
                    T.dma('sp', xres, [lambda: nc.sync.dma_start(out=xb, in_=xin[part][t * 128:(t + 1) * 128, :])],
                          writes=[xres])
                    rms_stage_a(xb, xres, t, t % 8)
                if t >= 1:
                    a1_tail(t - 1)
                if t >= 3:
                    inproj_tok(t - 3, 1536, 'uv%d' % (t - 3), evac='dve')
            for t in range(max(NT - 2, 0), NT):
                inproj_tok(t, 1536, 'uv%d' % t, evac='dve')

            T.barrier()
            T.dma('sp', 'mpool', [lambda: nc.sync.dma_start(out=mpool, in_=mpool_d)],
                  reads=[], writes=['mpool'] + ['xt%d' % i for i in range(3)] + ['xn%d' % i for i in range(4)] + ['hmul%d' % i for i in range(4)] + ['junk0', 'junk1'])

            if part == 0:
                ptiles = [(t, 0 if t == 0 else (2 if t == 15 else 1), t, 0, 128, 0) for t in range(16)]
            else:
                ptiles = [(t, 3 if t == 3 else (4 if t == 18 else 1), t - 3, 0, 128, 0) for t in range(3, 19)]
                ptiles.append((2, 1, 16, 64, 64, 0))
                ptiles.append((19, 1, 16, 0, 64, 64))
            ptiles.sort(key=lambda p_: p_[0])
            vdone = 0
            for pi, (t, var, slot, c0, ncol, sc0) in enumerate(ptiles):
                pp = bank(4).rearrange("p (g t) -> p g t", g=4)
                pw = bank(5).rearrange("p (g t) -> p g t", g=4)
                nb = [j for j in range(3) if 0 <= t + j - 1 < NT]

                def mm_pool():
                    for gi in range(4):
                        for n_, j in enumerate(nb):
                            ins = nc.tensor.matmul(pp[:, gi, :], lhsT=uv[:, t + j - 1, gi * 128:(gi + 1) * 128],
                                                   rhs=mpool[:, (var * 3 + j) * 4 + gi, :],
                                                   start=(n_ == 0), stop=(n_ == len(nb) - 1))
                    return ins
                T.op('pe', mm_pool, reads=['uv%d' % (t + j - 1) for j in nb] + ['mpool'], writes=['bank4'])
                mx = mixT[pi % 2]
                mres = 'mixT%d' % (pi % 2)
                T.op('dve', lambda: nc.vector.tensor_copy(out=mx, in_=pp), reads=['bank4'], writes=[mres])
                while vdone <= t - 1 and vdone < g.nkv:
                    inproj_tok(vdone, 1024, 'uv%d' % vdone)
                    vdone += 1

                def mm_wp():
                    for gi in range(4):
                        ins = nc.tensor.matmul(pw[:, gi, :], lhsT=wpool[:, gi, :], rhs=mx[:, gi, :], start=True, stop=True)
                    return ins
                T.op('pe', mm_wp, reads=[mres, 'wpool'], writes=['bank5'])
                psb = pscale[:, :].unsqueeze(2).to_broadcast([128, 4, ncol])
                T.op('dve', lambda: nc.vector.tensor_tensor(
                    out=catT[:, 4:8, slot * 128 + sc0:slot * 128 + sc0 + ncol], in0=pw[:, :, c0:c0 + ncol], in1=psb,
                    op=ALU.mult), reads=['bank5', 'pscale'], writes=['catP%d_%d' % (slot, sc0)])

            for t in range(vdone, g.nkv):
                inproj_tok(t, 1024, 'uv%d' % t)

            T.barrier()
            def qk_stages(c_, which, gi, seqn=0):
                dst, dres = (qc, 'qc') if which == 0 else (kc_, 'kc')
                bA = [0, 1, 6, 7][seqn % 4]
                bB = 4 + seqn % 2
                t0 = gi * 4
                nt = min(4, NT - t0)
                need = [t for t in range(t0, t0 + nt)
                        if ((t in g.qtiles) if which == 0 else (t < g.nkv))]
                t0, nt = need[0], need[-1] - need[0] + 1
                N = nt * 128
                cols = slice(t0 * 128, t0 * 128 + N)
                wcol = which * 512 + c_ * 128
                sqb = sq[seqn % 2]
                sres = 'sq%d' % (seqn % 2)
                rsb = rs[seqn % 2]
                rres = 'rs%d' % (seqn % 2)

                def st0():
                    def mm_qk():
                        for kc in range(8):
                            ins = nc.tensor.matmul(bank(bA)[:, 0:N], lhsT=win[:, kc, wcol:wcol + 128],
                                                   rhs=hT[:, kc, cols], start=(kc == 0), stop=(kc == 7))
                        return ins
                    T.op('pe', mm_qk, reads=['hT%d' % t for t in range(t0, t0 + nt)] + WIN_ALL, writes=['bank%d' % bA])

                def st1():
                    T.op('act', lambda: nc.scalar.activation(out=sqb[:, 0:N], in_=bank(bA)[:, 0:N], func=AF.Square),
                         reads=['bank%d' % bA], writes=[sres])

                def st2():
                    T.op('pe', lambda: nc.tensor.matmul(bank(bB)[:, 0:N], lhsT=bones[:], rhs=sqb[:, 0:N],
                                                        start=True, stop=True),
                         reads=[sres, 'bones'], writes=['bank%d' % bB])

                def st3():
                    T.op('act', lambda: nc.scalar.activation(out=rsb[:, 0:N], in_=bank(bB)[:, 0:N], func=AF.Ln,
                                                             scale=1.0 / HD, bias=epsb[:, 1:2]),
                         reads=['bank%d' % bB, 'epsb'], writes=[rres])
                    T.op('act', lambda: nc.scalar.activation(out=rsb[:, 0:N], in_=rsb[:, 0:N], func=AF.Exp, scale=-0.5),
                         reads=[rres], writes=[rres])

                def st4():
                    T.op('dve', lambda: nc.vector.scalar_tensor_tensor(
                        out=dst[:, cols], in0=bank(bA)[:, 0:N], scalar=qkg[:, which:which + 1], in1=rsb[:, 0:N],
                        op0=ALU.mult, op1=ALU.mult),
                        reads=['bank%d' % bA, 'qkg', rres], writes=['%s%d' % (dres, gi)])
                return [st0, st1, st2, st3, st4]

            def qk_group(c_, which, gi):
                for st in qk_stages(c_, which, gi):
                    st()

            ngrp = (NT + 3) // 4
            T.op('pool', lambda: nc.gpsimd.memset(vaug[:, :, :, 64:128], 1.0), writes=['vaug'])
            for c in range(4):
                T.dma('sp', 'T2', [lambda: nc.sync.dma_start(out=T2, in_=rpbG_d[2 * c:2 * c + 2].rearrange("h p f -> p h f"))],
                      writes=['T2', 'mpool', 'mixT0', 'mixT1'])
                T2v = T2.rearrange("p h (s q) -> p (h s) q", q=64)
                cmb = cmask[:, :].unsqueeze(1).to_broadcast([128, 32, 64])
                T.op('dve', lambda: nc.vector.tensor_tensor(out=T2v, in0=T2v, in1=cmb, op=ALU.add),
                     reads=['T2', 'cmask'], writes=['T2'])
                T3S = [(0, 0), (0, 1), (1, 0), (1, 1), (1, 2), (1, 3), (4, 1), (4, 2), (4, 3), (5, 3)]
                for k_, (o, i) in enumerate(T3S):
                    sp_ = 11 - 2 * o
                    mi = midx[(1, o, i)]
                    T.op('dve', lambda: nc.vector.tensor_scalar(
                        out=T3m[:, :, k_, :], in0=T2[:, :, (sp_ + i) * 64:(sp_ + i + 1) * 64],
                        scalar1=rmask[part][:, mi:mi + 1], scalar2=None, op0=ALU.add),
                        reads=['T2', 'rmask%d' % part], writes=['T3m'])
                for hh in range(2):
                    T.op('act', lambda: nc.scalar.copy(
                        out=vaug[:, 0:g.nkv, hh, 0:64], in_=uv[:, 0:g.nkv, (2 * c + hh) * 64:(2 * c + hh + 1) * 64]),
                        reads=['uv%d' % t for t in range(g.nkv)], writes=['vaug'])
                qgl = [qk_stages(c, which, gi, n_) for n_, (which, gi) in
                       enumerate([(w_, g_) for g_ in range(ngrp) for w_ in range(2)])]
                for step in range(len(qgl) + 4):
                    for k_ in reversed(range(5)):
                        idx = step - k_
                        if 0 <= idx < len(qgl):
                            qgl[idx][k_]()
                items = []
                border = list(range(len(g.blocks)))
                if part == 1:
                    border = [8] + list(range(8)) + [9]
                for bi in border:
                    blk = g.blocks[bi]
                    nrow = blk['nrow']
                    order = sorted(blk['pairs'], key=lambda p: (abs(p[2] - 2.4)))
                    plist = []
                    for (kt, sp, o) in order:
                        rows = [i for i in range(nrow) if many[(bi, o, i)]]
                        if not rows:
                            continue
                        plist.append((kt, sp, o, min(rows), max(rows) + 1))
                    assert plist[0][3] == 0 and plist[0][4] == nrow
                    for hh in range(2):
                        items.append(dict(bi=bi, blk=blk, hh=hh, nq=nrow * 64, plist=plist,
                                          units=[[p_] for p_ in plist]))
                ulist = []
                for it_i, it in enumerate(items):
                    for ui, unit in enumerate(it['units']):
                        ulist.append((it_i, ui, unit))
                SBANK = [[0], [1], [4], [5], [6], [7]]
                NSB = 6

                def emit_S(u):
                    it_i, ui, unit = ulist[u]
                    it = items[it_i]
                    blk, hh, bi = it['blk'], it['hh'], it['bi']
                    sbi = u % NSB
                    bks = SBANK[sbi]
                    interior = (blk['nrow'] == 4 and 1 <= bi <= 6)
                    qgs = sorted(set((blk['qcol0'] + i * 64) // 512 for i in range(blk['nrow'])))
                    kgs = sorted(set(kt // 4 for (kt, _, _, _, _) in unit))

                    def mm_s():
                        for j, (kt, sp, o, i0, i1) in enumerate(unit):
                            ins = nc.tensor.matmul(
                                bank(bks[j])[:, i0 * 64:i1 * 64],
                                lhsT=kc_[hh * 64:(hh + 1) * 64, kt * 128:(kt + 1) * 128],
                                rhs=qc[hh * 64:(hh + 1) * 64, blk['qcol0'] + i0 * 64:blk['qcol0'] + i1 * 64],
                                start=True, stop=True)
                        return ins
                    T.op('pe', mm_s, reads=['kc%d' % x for x in kgs] + ['qc%d' % x for x in qgs],
                         writes=['bank%d' % bks[j] for j in range(len(unit))] + ['S%d_%d' % (sbi, j) for j in range(len(unit))])
                    for j, (kt, sp, o, i0, i1) in enumerate(unit):
                        bkn = 'bank%d' % bks[j]
                        Sj = bank(bks[j])
                        Sv = Sj[:, i0 * 64:i1 * 64]
                        allok = all(mall[(bi, o, i)] for i in range(i0, i1))
                        if interior and (o, i0) in T3S:
                            k0 = T3S.index((o, i0))
                            assert all(T3S[k0 + d] == (o, i0 + d) for d in range(i1 - i0))
                            btab = T3m[:, hh, k0:k0 + (i1 - i0), :]
                            T.op('dve', lambda: nc.vector.tensor_tensor(
                                out=Sv.rearrange("p (a b) -> p a b", b=64), in0=Sv.rearrange("p (a b) -> p a b", b=64),
                                in1=btab, op=ALU.add),
                                reads=[bkn, 'T3m'], writes=['S%d_%d' % (sbi, j)])
                        elif allok:
                            T.op('dve', lambda: nc.vector.tensor_tensor(
                                out=Sv, in0=Sv, in1=T2[:, hh, (sp + i0) * 64:(sp + i1) * 64], op=ALU.add),
                                reads=[bkn, 'T2'], writes=['S%d_%d' % (sbi, j)])
                        else:
                            def bias_rows():
                                for i in range(i0, i1):
                                    mi = midx[(bi, o, i)]
                                    ins = nc.vector.scalar_tensor_tensor(
                                        out=Sj[:, i * 64:(i + 1) * 64],
                                        in0=Sj[:, i * 64:(i + 1) * 64],
                                        scalar=rmask[part][:, mi:mi + 1],
                                        in1=T2[:, hh, (sp + i) * 64:(sp + i + 1) * 64],
                                        op0=ALU.add, op1=ALU.add)
                                return ins
                            T.op('dve', bias_rows, reads=[bkn, 'T2', 'rmask%d' % part], writes=['S%d_%d' % (sbi, j)])
                        T.op('act', lambda: nc.scalar.activation(
                            out=PT[sbi][:, j * 256 + i0 * 64:j * 256 + i1 * 64], in_=Sv, func=AF.Exp),
                            reads=[bkn, 'S%d_%d' % (sbi, j)], writes=['PT%d_%d' % (sbi, j)])

                def emit_PV(u):
                    it_i, ui, unit = ulist[u]
                    it = items[it_i]
                    blk, hh, bi, nq = it['blk'], it['hh'], it['bi'], it['nq']
                    h = 2 * c + hh
                    sbi = u % NSB
                    ob = it_i % 2
                    O = bank(2 + ob)
                    ores = 'bank%d' % (2 + ob)
                    nun = len(it['units'])

                    def mm_pv():
                        for j, (kt, sp, o, i0, i1) in enumerate(unit):
                            first = (ui == 0 and j == 0)
                            last = (ui == nun - 1 and j == len(unit) - 1)
                            ins = nc.tensor.matmul(O[:, i0 * 64:i1 * 64], lhsT=vaug[:, kt, hh, :],
                                                   rhs=PT[sbi][:, j * 256 + i0 * 64:j * 256 + i1 * 64],
                                                   start=first, stop=last)
                        return ins
                    T.op('pe', mm_pv,
                         reads=['PT%d_%d' % (sbi, j) for j in range(len(unit))] + ['vaug'],
                         writes=[ores])
                    if ui == nun - 1:
                        rdb = rd[ob]
                        rdres = 'rd%d' % ob
                        T.op('act', lambda: nc.scalar.activation(out=rdb[0:64, 0:nq], in_=O[64:128, 0:nq], func=AF.Ln),
                             reads=[ores], writes=[rdres])
                        T.op('act', lambda: nc.scalar.activation(out=rdb[0:64, 0:nq], in_=rdb[0:64, 0:nq], func=AF.Exp,
                                                                 scale=-1.0), reads=[rdres], writes=[rdres])
                        T.op('dve', lambda: nc.vector.tensor_tensor(
                            out=catT[hh * 64:(hh + 1) * 64, c, blk['scol0']:blk['scol0'] + nq],
                            in0=O[0:64, 0:nq], in1=rdb[0:64, 0:nq], op=ALU.mult),
                            reads=[ores, rdres], writes=['catA%d_%d_%d' % (c, hh, bi)])

                LA = 5
                U = len(ulist)
                pend = []
                if False:
                    lastq = {}
                    lastk = {}
                    for u_, (it_i, ui, unit) in enumerate(ulist):
                        blk_ = items[it_i]['blk']
                        for i in range(blk_['nrow']):
                            lastq[(blk_['qcol0'] + i * 64) // 512] = u_
                        for (kt, _, _, _, _) in unit:
                            lastk[kt // 4] = u_
                    for gi in range(ngrp):
                        pend.append((lastq.get(gi, -1) + 1, 0, gi))
                        pend.append((lastk.get(gi, -1) + 1, 1, gi))
                    pend.sort()
                active = []
                nxt = 4
                for u in range(U + LA):
                    if u < U:
                        emit_S(u)
                    if u - LA >= 0:
                        emit_PV(u - LA)
                    if active:
                        if u >= nxt:
                            active.pop(0)()
                            nxt = u + 2
                    elif pend and u >= nxt and pend[0][0] <= u - 2:
                        _, which_, gi_ = pend.pop(0)
                        active = qk_stages(c + 1, which_, gi_)
                        active.pop(0)()
                        nxt = u + 2
                for st in active:
                    st()
                for (_, which_, gi_) in pend:
                    qk_group(c + 1, which_, gi_)

            T.barrier()
            T.dma('pool', 'wout', [(lambda kc=kc: nc.gpsimd.dma_start(out=wout[:, kc, :], in_=wout_v[:, kc, :]))
                                   for kc in range(8)], writes=['wout', 'vaug'])
            CAT_ALL = None
            for s in range(NS):
                t = g.qtiles[s]
                xres = 'x1_%d' % s
                T.dma('sp', xres, [lambda: nc.sync.dma_start(out=x1[:, s, :], in_=xin[part][t * 128:(t + 1) * 128, :])],
                      writes=[xres])
                bk = 2 * (s % 2)

                def mm_o():
                    for half in range(2):
                        for kc in range(8):
                            ins = nc.tensor.matmul(bank(bk + half), lhsT=catT[:, kc, s * 128:(s + 1) * 128],
                                                   rhs=wout[:, kc, half * 512:(half + 1) * 512],
                                                   start=(kc == 0), stop=(kc == 7))
                    return ins
                T.op('pe', mm_o, reads=['wout'], writes=['bank%d' % bk, 'bank%d' % (bk + 1)])
                ot = otmp[s % 2]
                otres = 'otmp%d' % (s % 2)
                T.op('dve', lambda: nc.vector.tensor_tensor(out=ot, in0=ps[:, bk * 512:bk * 512 + 1024],
                                                            in1=gate[:, b * 2 + 0, :], op=ALU.mult),
                     reads=['bank%d' % bk, 'bank%d' % (bk + 1), 'gate'], writes=[otres])
                T.op('pool', lambda: nc.gpsimd.tensor_tensor(out=x1[:, s, :], in0=x1[:, s, :], in1=ot, op=ALU.add),
                     reads=[otres, xres], writes=[xres])
            T.barrier()

            h2T = abf(68, [8, 2050])
            WU0 = 103.0
            WD0 = 147.0
            wup_u = [abf(WU0 + 4 * j, [8, 256]) for j in range(11)]
            wdn_u = [abf(WD0 + 2 * j, [1024]) for j in range(11)]
            G = abf(169, [13, 256])
            cg = [af32(175.5 + i, [256]) for i in range(3)]
            cv = [af32(178.5 + i, [256]) for i in range(3)]
            sg = [af32(181.5 + i, [256]) for i in range(3)]
            xn = [abf(169 + 2 * i, [1024]) for i in range(3)]
            junk = [abf(183, [1024]), abf(100.5, [1024])]
            dtmp_b = [af32(100.5, [512]), af32(184.5, [512])]
            hmul = [af32(175 + 4 * i, [8, 128]) for i in range(2)]

            def load_up(half, j):
                jj = half * 11 + j
                T.dma('pool', 'wupu%d' % j, [
                    lambda: nc.gpsimd.dma_start(out=wup_u[j][:, :, 0:128], in_=wup_v[:, :, jj * 128:(jj + 1) * 128]),
                    lambda: nc.gpsimd.dma_start(out=wup_u[j][:, :, 128:256],
                                                in_=wup_v[:, :, FF + jj * 128:FF + (jj + 1) * 128])],
                    writes=['wupu%d' % j])

            def load_dn(half, j):
                jj = half * 11 + j
                T.dma('pool', 'wdnu%d' % j, [lambda: nc.gpsimd.dma_start(out=wdn_u[j], in_=wdn_v[:, jj, :])],
                      writes=['wdnu%d' % j])

            for j in range(11):
                load_up(0, j)
            for j in range(11):
                load_dn(0, j)
            if part == 0:
                T.op('dve', lambda: nc.vector.memset(h2T[:, :, 0:1], 0.0), writes=['h2halo0'])
                T.op('dve', lambda: nc.vector.memset(h2T[:, :, 2049:2050], 0.0), writes=['h2halo1'])
            def n2_tail(s):
                pT, bres = rms_stage_b(x1[:, s, :], 'x1_%d' % s, s, 8 + s % 8)
                hm = hmul[s % len(hmul)]
                hres = 'hmul%d' % (s % len(hmul))
                a2b = a2[:, :, b:b + 1].to_broadcast([128, 8, 128])
                s2b = sh2[:, :, b:b + 1].to_broadcast([128, 8, 128])
                if s < 16:
                    def mod2():
                        for kc in range(8):
                            ins = nc.vector.scalar_tensor_tensor(
                                out=h2T[:, kc, 1 + s * 128:1 + (s + 1) * 128], in0=pT[:, kc, :],
                                scalar=a2[:, kc, b:b + 1], in1=sh2[:, kc, b:b + 1].to_broadcast([128, 128]),
                                op0=ALU.mult, op1=ALU.add)
                        return ins
                    T.op('dve', mod2, reads=[bres, 'amod', 'modF'], writes=['h2T%d' % s])
                else:
                    T.op('dve', lambda: nc.vector.tensor_tensor(out=hm, in0=pT, in1=a2b, op=ALU.mult),
                         reads=[bres, 'amod'], writes=[hres])
                    s2c = sh2[:, :, b:b + 1]
                    T.op('pool', lambda: nc.gpsimd.tensor_tensor(out=hm[:, :, 63:65], in0=hm[:, :, 63:65],
                                                                 in1=s2c.to_broadcast([128, 8, 2]), op=ALU.add),
                         reads=[hres, 'modF'], writes=[hres])
                    T.op('pool', lambda: nc.gpsimd.tensor_scalar(out=h2T[:, :, 0:1], in0=hm[:, :, 63:64],
                                                                 scalar1=halov[:, 0:1], scalar2=None, op0=ALU.mult),
                         reads=[hres, 'halov'], writes=['h2halo0'])
                    T.op('pool', lambda: nc.gpsimd.tensor_scalar(out=h2T[:, :, 2049:2050], in0=hm[:, :, 64:65],
                                                                 scalar1=halov[:, 1:2], scalar2=None, op0=ALU.mult),
                         reads=[hres, 'halov'], writes=['h2halo1'])

            for s in range(NS + 1):
                if s < NS:
                    rms_stage_a(x1[:, s, :], 'x1_%d' % s, s, 8 + s % 8)
                if s >= 1:
                    n2_tail(s - 1)
            T.barrier()

            cw = convw[:, :].rearrange("p (c k) -> p c k", k=3)
            NG = 8
            seq = [(half, grp, j) for half in range(2) for grp in range(NG) for j in range(11)]
            nseq = len(seq)

            def gslot(half, grp, j):
                return (11 * (half * NG + grp) + j) % 13

            def hreads_of(grp):
                r = ['h2T%d' % (2 * grp), 'h2T%d' % (2 * grp + 1), 'h2halo0', 'h2halo1']
                if grp > 0:
                    r.append('h2T%d' % (2 * grp - 1))
                if grp < NG - 1:
                    r.append('h2T%d' % (2 * grp + 2))
                return r

            def emit_up(i):
                half, grp, j = seq[i]
                st_ = i % 3
                pg, pv = bank(2 * st_), bank(2 * st_ + 1)
                g0 = grp * 256

                def mm_up():
                    for which, pbk in ((0, pg), (1, pv)):
                        for kc in range(8):
                            ins = nc.tensor.matmul(pbk[:, 0:258], lhsT=wup_u[j][:, kc, which * 128:(which + 1) * 128],
                                                   rhs=h2T[:, kc, g0:g0 + 258], start=(kc == 0), stop=(kc == 7))
                    return ins
                T.op('pe', mm_up, reads=hreads_of(grp) + ['wupu%d' % j],
                     writes=['bank%d' % (2 * st_), 'bank%d' % (2 * st_ + 1)])
                if half == 0 and grp == NG - 1:
                    load_up(1, j)
            def emit_tap1(i):
                half, grp, j = seq[i]
                st_ = i % 3
                pg, pv = bank(2 * st_), bank(2 * st_ + 1)
                jj = half * 11 + j
                chg, chv = jj, 22 + jj
                T.op('act', lambda: nc.scalar.activation(out=cg[st_], in_=pg[:, 1:257], func=AF.Identity,
                                                         scale=cw[:, chg, 1:2], bias=convb[:, chg:chg + 1]),
                     reads=['bank%d' % (2 * st_), 'convw', 'convb'], writes=['cg%d' % st_])
                T.op('act', lambda: nc.scalar.activation(out=cv[st_], in_=pv[:, 1:257], func=AF.Identity,
                                                         scale=cw[:, chv, 1:2], bias=convb[:, chv:chv + 1]),
                     reads=['bank%d' % (2 * st_ + 1), 'convw', 'convb'], writes=['cv%d' % st_])

            def emit_taps(i):
                half, grp, j = seq[i]
                st_ = i % 3
                pg, pv = bank(2 * st_), bank(2 * st_ + 1)
                jj = half * 11 + j
                chg, chv = jj, 22 + jj
                for (buf, pbk, ch, nm, bk_) in ((cg[st_], pg, chg, 'cg%d' % st_, 2 * st_),
                                                (cv[st_], pv, chv, 'cv%d' % st_, 2 * st_ + 1)):
                    T.op('dve', lambda: nc.vector.scalar_tensor_tensor(
                        out=buf, in0=pbk[:, 0:256], scalar=cw[:, ch, 0:1], in1=buf, op0=ALU.mult, op1=ALU.add),
                        reads=['bank%d' % bk_, nm, 'convw'], writes=[nm])
                    T.op('dve', lambda: nc.vector.scalar_tensor_tensor(
                        out=buf, in0=pbk[:, 2:258], scalar=cw[:, ch, 2:3], in1=buf, op0=ALU.mult, op1=ALU.add),
                        reads=['bank%d' % bk_, nm, 'convw'], writes=[nm])
                T.op('act', lambda: nc.scalar.activation(out=sg[st_], in_=cg[st_], func=AF.Silu),
                     reads=['cg%d' % st_], writes=['sg%d' % st_])
                gs_ = gslot(half, grp, j)
                T.op('pool', lambda: nc.gpsimd.tensor_tensor(out=G[:, gs_, :], in0=cv[st_], in1=sg[st_], op=ALU.mult),
                     reads=['cv%d' % st_, 'sg%d' % st_], writes=['G%d' % gs_])

            def emit_down(half, grp):
                for tt in range(2):
                    s_ = 2 * grp + tt
                    xres = 'x1_%d' % s_
                    for h2 in range(2):
                        def mm_dn():
                            for j in range(11):
                                ins = nc.tensor.matmul(bank(6 + h2), lhsT=G[:, gslot(half, grp, j), tt * 128:(tt + 1) * 128],
                                                       rhs=wdn_u[j][:, h2 * 512:(h2 + 1) * 512],
                                                       start=(j == 0), stop=(j == 10))
                            return ins
                        T.op('pe', mm_dn, reads=['G%d' % gslot(half, grp, j) for j in range(11)] + ['wdnu%d' % j for j in range(11)],
                             writes=['bank%d' % (6 + h2)])
                        dt_ = dtmp_b[h2]
                        dres = 'dtmpb%d' % h2
                        T.op('dve', lambda: nc.vector.tensor_tensor(
                            out=dt_, in0=bank(6 + h2), in1=gate[:, b * 2 + 1, h2 * 512:(h2 + 1) * 512], op=ALU.mult),
                            reads=['bank%d' % (6 + h2), 'gate'], writes=[dres])
                        T.op('pool', lambda: nc.gpsimd.tensor_tensor(
                            out=x1[:, s_, h2 * 512:(h2 + 1) * 512], in0=x1[:, s_, h2 * 512:(h2 + 1) * 512], in1=dt_,
                            op=ALU.add), reads=[dres, xres], writes=[xres])
                    if half == 1:
                        tok = T.dma('sp', 'st%d' % s_,
                                    [lambda: nc.sync.dma_start(out=yout[part][s_ * 128:(s_ + 1) * 128, :], in_=x1[:, s_, :])],
                                    reads=[xres])
                        store_toks.append(tok)
                if half == 0 and grp == NG - 1:
                    for j in range(11):
                        load_dn(1, j)

            for i in range(nseq + 2):
                if i < nseq:
                    emit_up(i)
                k = i - 1
                if 0 <= k < nseq:
                    emit_taps(k)
                if 0 <= k < nseq and seq[k][2] == 1 and k >= 12:
                    ph, pg_, _ = seq[k - 12]
                    emit_down(ph, pg_)
                if i < nseq:
                    emit_tap1(i)
            emit_down(1, NG - 1)
            T.barrier()

        for tok in store_toks:
            T._wait('sp', tok)
        T.barrier()
    return nc


_NC_CACHE = {}


def kernel(x_prompt, x_sample, c_prompt, c_sample, w_ada, b_ada, norm1_g, norm2_g,
           w_in, q_norm_g, k_norm_g, rpb, w_pool, pool_scale, w_out, w_up,
           conv_w, conv_b, w_down):
    f32 = np.float32
    A = lambda a: np.ascontiguousarray(np.asarray(a, dtype=f32))
    x_prompt, x_sample = A(x_prompt), A(x_sample)
    st = static_tables()
    dr, dc = rpb_gather_index()
    rpbG = A(rpb)[0][:, dr, dc].reshape(NH, 128, 1024)
    fm = lambda v: np.ascontiguousarray(A(v).reshape(-1, 128).T)
    ngv = np.concatenate([fm(A(norm1_g)[0]), fm(A(norm2_g)[0])], axis=1)
    qkg = np.stack([np.tile(A(q_norm_g)[0], 2), np.tile(A(k_norm_g)[0], 2)], axis=1)
    pscale = fm(A(pool_scale)[0])
    cw = A(conv_w)[0]
    convw = np.ascontiguousarray(cw.reshape(3, 44, 128).transpose(2, 1, 0)).reshape(128, 132)
    convb = fm(A(conv_b)[0])
    shared = dict(
        w_ada=A(w_ada)[0], b_ada2=np.ascontiguousarray(np.broadcast_to(A(b_ada)[0][None, :], (2, 6 * D))),
        ng=ngv, w_in=A(w_in)[0], qkg=np.ascontiguousarray(qkg), rpbG=np.ascontiguousarray(rpbG),
        cmask=st['cmask'], w_pool=A(w_pool)[0], pscale=pscale, w_out=A(w_out)[0], w_up=A(w_up)[0],
        convw=convw, convb=convb, w_down=A(w_down)[0], ident=st['ident'], bones=st['bones'], sel=st['sel'],
        i2=st['i2'])
    in_maps = []
    for core in range(N_CORES):
        sb_, ch = core // 4, core % 4
        R0 = 32 * ch
        xs = np.zeros((22, 2, 64, D), f32)
        xseq = x_sample[sb_].reshape(128, 64, D)
        for t in range(21):
            for hh in range(2):
                r = R0 - 6 + 2 * t + hh
                if 0 <= r < 128:
                    xs[t, hh] = xseq[r]
        if R0 - 1 >= 0:
            xs[21, 0] = xseq[R0 - 1]
        if R0 + 32 < 128:
            xs[21, 1] = xseq[R0 + 32]
        cT = np.stack([fm(A(c_prompt)[core]), fm(A(c_sample)[sb_])], axis=2).reshape(128, 16)
        m = dict(shared)
        m.update(host_tables(core))
        m.update(xp=x_prompt[core], xs=xs.reshape(22 * 128, D), cT=np.ascontiguousarray(cT))
        in_maps.append(m)
    if 'nc' not in _NC_CACHE:
        _NC_CACHE['nc'] = build_nc()
    nc = _NC_CACHE['nc']
    res = run_bass_kernel_spmd(nc, in_maps, core_ids=list(range(N_CORES)))
    yp = np.stack([res.results[i]['yp'] for i in range(N_CORES)], axis=0).astype(f32)
    ys = np.stack([res.results[i]['ys'] for i in range(N_CORES)], axis=0).reshape(2, 4 * 2048, D).astype(f32)
    return yp, ys
```

```python
import numpy as np
import ml_dtypes
from contextlib import ExitStack
import concourse.bass as bass
import concourse.mybir as mybir
from concourse.bass_utils import run_bass_kernel_spmd

F32 = mybir.dt.float32
BF16 = mybir.dt.bfloat16
AF = mybir.ActivationFunctionType
ALU = mybir.AluOpType

D = 1024
NKC = 8
NH = 8
HD = 64
FF = 2816
NJ = 22
GRID_W = 64
WIN_COLS = 16
EPS = 1e-6
NEG = -30000.0
POOL_WINDOWS = (2, 4, 8, 16)
N_CORES = 8


class PartGeom:
    pass


def make_geom(part):
    g = PartGeom()
    g.part = part
    if part == 0:
        g.ntile = 16
        g.nkv = 16
        g.row0 = 0
        g.qtiles = list(range(16))
        g.nslot = 16
        g.seq_rows = 32
    else:
        g.ntile = 22
        g.nkv = 21
        g.row0 = -6
        g.qtiles = list(range(3, 19)) + [21]
        g.nslot = 17
        g.seq_rows = 128
    blocks = []
    for m in range(8):
        if part == 0:
            qt0 = 2 * m
            pairs = [(2 * m - 2 + o, 11 - 2 * o, o) for o in range(6) if 0 <= 2 * m - 2 + o <= 15]
            slot0 = 2 * m
        else:
            qt0 = 3 + 2 * m
            pairs = [(2 * m + 1 + o, 11 - 2 * o, o) for o in range(6)]
            slot0 = 2 * m
        blocks.append(dict(qcol0=qt0 * 128, nrow=4, scol0=slot0 * 128, qrow0=4 * m, pairs=pairs))
    if part == 1:
        blocks.append(dict(qcol0=21 * 128, nrow=1, scol0=16 * 128, qrow0=-1,
                           pairs=[(o, 12 - 2 * o, o) for o in range(5)]))
        blocks.append(dict(qcol0=21 * 128 + 64, nrow=1, scol0=16 * 128 + 64, qrow0=32,
                           pairs=[(17 + o, 11 - 2 * o, o) for o in range(4)]))
    g.blocks = blocks
    return g


def key_row_of_tile(g, t):
    return g.row0 + 2 * t


def row_valid(seq_rows, R0, qr_local, kr_local):
    gq = R0 + qr_local
    gk = R0 + kr_local
    if gq < 0 or gq >= seq_rows:
        return True
    if gk < 0 or gk >= seq_rows:
        return False
    rs = min(max(gq - 4, 0), seq_rows - 8)
    return rs <= gk <= rs + 7


def part_R0(part, core):
    return 0 if part == 0 else 32 * (core % 4)


def mask_layout(g):
    idx = {}
    n = 0
    anyv = {}
    allv = {}
    for bi, blk in enumerate(g.blocks):
        for (kt, sp, o) in blk['pairs']:
            kr0 = key_row_of_tile(g, kt)
            for i in range(blk['nrow']):
                idx[(bi, o, i)] = n
                n += 1
                a = False
                al = True
                for core in range(N_CORES):
                    R0 = part_R0(g.part, core)
                    for half in range(2):
                        v = row_valid(g.seq_rows, R0, blk['qrow0'] + i, kr0 + half)
                        a = a or v
                        al = al and v
                anyv[(bi, o, i)] = a
                allv[(bi, o, i)] = al
    return idx, n, anyv, allv


GEOMS = [make_geom(0), make_geom(1)]
MASKS = [mask_layout(g) for g in GEOMS]


def pool_matrix(T0, TS):
    M = np.zeros((3, 4, 128, 128), np.float32)
    for gi, w in enumerate(POOL_WINDOWS):
        for t in range(128):
            tout = T0 + t
            if tout < 0 or tout >= TS:
                continue
            lo = min(max(tout - w // 2, 0), TS)
            hi = min(max(tout - w // 2 + w, 0), TS)
            cnt = hi - lo
            for tin in range(lo, hi):
                rel = tin - T0
                j = rel // 128 + 1
                M[j, gi, rel % 128, t] += 1.0 / cnt
            M[1, gi, t, t] -= 1.0
    return M


def host_tables(core):
    ch = core % 4
    R0s = 32 * ch
    tabs = {}
    mp = np.stack([
        pool_matrix(0, 2048),
        pool_matrix(4096, 8192 * 4),
        pool_matrix(2048 - 128, 2048),
        pool_matrix(R0s * 64, 8192),
        pool_matrix((R0s + 30) * 64, 8192),
    ])
    tabs['mpool'] = np.ascontiguousarray(mp.transpose(3, 0, 1, 2, 4).reshape(128, 60, 128)).astype(ml_dtypes.bfloat16)
    for p in range(2):
        g = GEOMS[p]
        idx, n, _, _ = MASKS[p]
        R0 = part_R0(p, core)
        rm = np.zeros((128, n), np.float32)
        for bi, blk in enumerate(g.blocks):
            for (kt, sp, o) in blk['pairs']:
                kr0 = key_row_of_tile(g, kt)
                for i in range(blk['nrow']):
                    for half in range(2):
                        if not row_valid(g.seq_rows, R0, blk['qrow0'] + i, kr0 + half):
                            rm[half * 64:(half + 1) * 64, idx[(bi, o, i)]] = NEG
        tabs['rmask%d' % p] = rm
    hv = np.zeros((128, 2), np.float32)
    hv[:, 0] = 1.0 if ch > 0 else 0.0
    hv[:, 1] = 1.0 if ch < 3 else 0.0
    tabs['halov'] = hv
    return tabs


def static_tables():
    cols = np.arange(GRID_W)
    cs = np.clip(cols - WIN_COLS // 2, 0, GRID_W - WIN_COLS)
    cm = np.zeros((128, 64), np.float32)
    for kc in range(64):
        for qc in range(64):
            ok = cs[qc] <= kc < cs[qc] + WIN_COLS
            if not ok:
                cm[kc, qc] = NEG
                cm[64 + kc, qc] = NEG
    ident = np.eye(128, dtype=np.float32)
    bones = np.zeros((128, 128), np.float32)
    bones[:64, :64] = 1
    bones[64:, 64:] = 1
    sel = np.zeros((2, 2, 128), np.float32)
    sel[0, 0, :] = 1
    sel[1, 1, :] = 1
    return dict(cmask=cm, ident=ident.astype(ml_dtypes.bfloat16), bones=bones.astype(ml_dtypes.bfloat16),
                sel=sel.reshape(2, 256), i2=np.eye(2, dtype=np.float32))


def rpb_gather_index():
    dr = np.zeros((128, 16, 64), np.int64)
    dc = np.zeros((128, 16, 64), np.int64)
    for p in range(128):
        half, kc = p // 64, p % 64
        for s in range(16):
            delta = 7 - s
            for qc in range(64):
                dr[p, s, qc] = min(max(delta + half + 7, 0), 14)
                dc[p, s, qc] = min(max(kc - qc + 15, 0), 30)
    return dr, dc


class Trk:
    def __init__(self, nc, stack):
        self.nc = nc
        self.stack = stack
        self.engs = {'pe': nc.tensor, 'act': nc.scalar, 'dve': nc.vector, 'pool': nc.gpsimd, 'sp': nc.sync}
        self.esem = {k: stack.enter_context(nc.semaphore('e_' + k)) for k in ('pe', 'act', 'dve', 'pool')}
        self.ecnt = {k: 0 for k in self.esem}
        self.waited = {}
        self.res = {}
        self.dsem = {}
        self.dcnt = {}
        self.nops = 0

    def _sem(self, tok):
        kind, key, val = tok
        return self.esem[key] if kind == 'e' else self.dsem[key]

    def _wait(self, eng, tok):
        if tok is None:
            return
        kind, key, val = tok
        if kind == 'e' and key == 'pe' and eng == 'pe':
            return
        wk = (eng, kind, key)
        if self.waited.get(wk, 0) >= val:
            return
        self.engs[eng].wait_ge(self._sem(tok), val)
        self.waited[wk] = val

    def deps(self, eng, reads, writes):
        for r in reads:
            e = self.res.get(r)
            if e:
                self._wait(eng, e[0])
        for w in writes:
            e = self.res.get(w)
            if e:
                self._wait(eng, e[0])
                for t in e[1].values():
                    self._wait(eng, t)

    def commit(self, tok, reads, writes):
        for r in reads:
            e = self.res.setdefault(r, [None, {}])
            k = (tok[0], tok[1])
            if k not in e[1] or e[1][k][2] < tok[2]:
                e[1][k] = tok
        for w in writes:
            self.res[w] = [tok, {}]

    def op(self, eng, fn, reads=(), writes=()):
        self.deps(eng, reads, writes)
        ins = fn()
        self.ecnt[eng] += 1
        ins.then_inc(self.esem[eng], 1)
        tok = ('e', eng, self.ecnt[eng])
        self.commit(tok, reads, writes)
        self.nops += 1

    def dma(self, eng, key, fns, reads=(), writes=()):
        if key not in self.dsem:
            self.dsem[key] = self.stack.enter_context(self.nc.semaphore('d_' + key))
            self.dcnt[key] = 0
        if self.dcnt[key] > 0:
            self._wait(eng, ('d', key, self.dcnt[key]))
        self.deps(eng, reads, writes)
        for fn in fns:
            ins = fn()
            ins.then_inc(self.dsem[key], 16)
            self.dcnt[key] += 16
        tok = ('d', key, self.dcnt[key])
        self.commit(tok, reads, writes)
        return tok

    def barrier(self):
        toks = [('e', k, self.ecnt[k]) for k in self.esem if self.ecnt[k] > 0]
        toks += [('d', k, self.dcnt[k]) for k in self.dsem if self.dcnt[k] > 0]
        for eng in ('pe', 'act', 'dve', 'pool', 'sp'):
            for t in toks:
                self._wait(eng, t)


def build_nc():
    nc = bass.Bass("TRN2", target_bir_lowering=False)

    def din(name, shape, dt=F32):
        return nc.dram_tensor(name, list(shape), dt, kind="ExternalInput").ap()

    xin = [din("xp", [2048, D]), din("xs", [22 * 128, D])]
    yout = [nc.dram_tensor("yp", [2048, D], F32, kind="ExternalOutput").ap(),
            nc.dram_tensor("ys", [2048, D], F32, kind="ExternalOutput").ap()]
    cT_d = din("cT", [128, 16])
    wada_d = din("w_ada", [D, 6 * D])
    bada_d = din("b_ada2", [2, 6 * D])
    ng_d = din("ng", [128, 16])
    win_d = din("w_in", [D, 2048])
    qkg_d = din("qkg", [128, 2])
    rpbG_d = din("rpbG", [NH, 128, 1024])
    cmask_d = din("cmask", [128, 64])
    wpool_d = din("w_pool", [4, 128, 128])
    pscale_d = din("pscale", [128, 4])
    wout_d = din("w_out", [D, D])
    wup_d = din("w_up", [D, 2 * FF])
    convw_d = din("convw", [128, 44 * 3])
    convb_d = din("convb", [128, 44])
    wdn_d = din("w_down", [FF, D])
    mpool_d = din("mpool", [128, 60, 128], BF16)
    rmask_d = [din("rmask0", [128, MASKS[0][1]]), din("rmask1", [128, MASKS[1][1]])]
    halov_d = din("halov", [128, 2])
    ident_d = din("ident", [128, 128], BF16)
    bones_d = din("bones", [128, 128], BF16)
    sel_d = din("sel", [2, 256])
    i2_d = din("i2", [2, 2])

    wada_v = wada_d.rearrange("(kc p) e -> p kc e", p=128)
    win_v = win_d.rearrange("(kc p) e -> p kc e", p=128)
    wout_v = wout_d.rearrange("(kc p) e -> p kc e", p=128)
    wup_v = wup_d.rearrange("(kc p) e -> p kc e", p=128)
    wdn_v = wdn_d.rearrange("(j p) e -> p j e", p=128)
    wpool_v = wpool_d.rearrange("g c d -> c g d")

    stack = ExitStack()
    with stack:
        T = Trk(nc, stack)

        def sb(name, shape, dt):
            return stack.enter_context(nc.sbuf_tensor("s_" + name, list(shape), dt))

        ident = sb("ident", [128, 128], BF16)
        bones = sb("bones", [128, 128], BF16)
        ones_bf = sb("ones_bf", [128, 64], BF16)
        sel = sb("sel", [2, 256], F32)
        i2 = sb("i2", [2, 2], F32)
        cT = sb("cT", [128, 16], F32)
        scT = sb("scT", [128, 16], BF16)
        ng = sb("ng", [128, 16], F32)
        qkg = sb("qkg", [128, 2], F32)
        pscale = sb("pscale", [128, 4], F32)
        convw = sb("convw", [128, 44 * 3], F32)
        convb = sb("convb", [128, 44], F32)
        halov = sb("halov", [128, 2], F32)
        cmask = sb("cmask", [128, 64], F32)
        modF = sb("modF", [128, 64], F32)
        amod = sb("amod", [128, 32], F32)
        gate = sb("gate", [128, 4, 1024], F32)
        rmask = [sb("rmask0", [128, MASKS[0][1]], F32), sb("rmask1", [128, MASKS[1][1]], F32)]
        stat = sb("stat", [128, 64], F32)
        AR_ELEMS = 95488
        arena = sb("arena", [128, AR_ELEMS], BF16)
        ps = stack.enter_context(nc.psum_tensor("ps", [128, 4096], F32))

        KB = 512

        def abf(off_kb, shape):
            n = int(np.prod(shape))
            o = int(off_kb * KB)
            v = arena[:, o:o + n]
            if len(shape) == 2:
                v = v.rearrange("p (a b) -> p a b", a=shape[0])
            elif len(shape) == 3:
                v = v.rearrange("p (a b c) -> p a b c", a=shape[0], b=shape[1])
            return v

        def af32(off_kb, shape):
            n = int(np.prod(shape))
            o = int(off_kb * KB)
            v = arena[:, o:o + 2 * n].bitcast(F32)
            if len(shape) == 2:
                v = v.rearrange("p (a b) -> p a b", a=shape[0])
            elif len(shape) == 3:
                v = v.rearrange("p (a b c) -> p a b c", a=shape[0], b=shape[1])
            return v

        def bank(i, n=512, dt=F32):
            v = ps[:, i * 512:i * 512 + n]
            return v

        def ld(dst, src, key, eng='sp'):
            T.dma(eng, key, [lambda: T.engs[eng].dma_start(out=dst, in_=src)], writes=[key])

        ld(ident[:], ident_d, 'ident')
        ld(bones[:], bones_d, 'bones')
        ld(sel[:], sel_d, 'sel')
        ld(i2[:], i2_d, 'i2')
        ld(cT[:], cT_d, 'cT')
        ld(ng[:], ng_d, 'ng')
        ld(qkg[:], qkg_d, 'qkg')
        ld(pscale[:], pscale_d, 'pscale')
        ld(convw[:], convw_d, 'convw')
        ld(convb[:], convb_d, 'convb')
        ld(halov[:], halov_d, 'halov')
        ld(cmask[:], cmask_d, 'cmask')
        ld(rmask[0][:], rmask_d[0], 'rmask0')
        ld(rmask[1][:], rmask_d[1], 'rmask1')
        T.op('dve', lambda: nc.vector.memset(ones_bf[:], 1.0), writes=['ones_bf'])
        T.op('dve', lambda: nc.vector.tensor_scalar(out=qkg[:, 0:1], in0=qkg[:, 0:1], scalar1=0.125, scalar2=None,
                                                    op0=ALU.mult), reads=['qkg'], writes=['qkg'])

        T.op('act', lambda: nc.scalar.activation(out=scT[:], in_=cT[:], func=AF.Silu), reads=['cT'], writes=['scT'])
        wada_sb = [abf(0, [8, 1024]), abf(16, [8, 1024])]
        modrow = af32(32, [1024])
        brow = af32(36, [1024])
        psF = bank(6, 64)
        fm_blocks = {0: 0, 1: 1, 3: 2, 4: 3}
        for blk in range(6):
            wb = wada_sb[blk % 2]
            wkey = 'wada%d' % (blk % 2)
            T.dma('pool', wkey, [
                (lambda kc=kc: nc.gpsimd.dma_start(out=wb[:, kc, :], in_=wada_v[:, kc, blk * 1024:(blk + 1) * 1024]))
                for kc in range(8)], writes=[wkey])
            T.dma('sp', 'brow', [lambda: nc.sync.dma_start(out=brow[0:2, :], in_=bada_d[:, blk * 1024:(blk + 1) * 1024])],
                  writes=['brow'])

            def mm_ada():
                for half in range(2):
                    for kc in range(8):
                        ins = nc.tensor.matmul(bank(half)[0:2, :], lhsT=scT[:, 2 * kc:2 * kc + 2],
                                               rhs=wb[:, kc, half * 512:(half + 1) * 512],
                                               start=(kc == 0), stop=(kc == 7))
                return ins
            T.op('pe', mm_ada, reads=['scT', wkey], writes=['bank0', 'bank1'])
            T.op('dve', lambda: nc.vector.tensor_tensor(out=modrow[0:2, :], in0=ps[0:2, 0:1024], in1=brow[0:2, :],
                                                        op=ALU.add),
                 reads=['bank0', 'bank1', 'brow'], writes=['modrow'])
            if blk in fm_blocks:
                fb = fm_blocks[blk]

                def mm_fm():
                    for kc in range(8):
                        ins = nc.tensor.matmul(psF[:, (fb * 8 + kc) * 2:(fb * 8 + kc) * 2 + 2],
                                               lhsT=modrow[0:2, kc * 128:(kc + 1) * 128], rhs=i2[0:2, :],
                                               start=True, stop=True)
                    return ins
                T.op('pe', mm_fm, reads=['modrow', 'i2'], writes=['bank6'])
            else:
                gi = 0 if blk == 2 else 1
                for b in range(2):
                    def mm_g():
                        for half in range(2):
                            ins = nc.tensor.matmul(bank(2 + half), lhsT=sel[0:2, b * 128:(b + 1) * 128],
                                                   rhs=modrow[0:2, half * 512:(half + 1) * 512],
                                                   start=True, stop=True)
                        return ins
                    T.op('pe', mm_g, reads=['modrow', 'sel'], writes=['bank2', 'bank3'])
                    T.op('act', lambda: nc.scalar.copy(out=gate[:, b * 2 + gi, :], in_=ps[:, 1024:2048]),
                         reads=['bank2', 'bank3'], writes=['gate'])
        T.op('dve', lambda: nc.vector.tensor_copy(out=modF[:], in_=psF), reads=['bank6'], writes=['modF'])
        for which in range(2):
            scblk = 1 if which == 0 else 3
            T.op('dve', lambda: nc.vector.tensor_scalar(out=amod[:, which * 16:(which + 1) * 16],
                                                        in0=modF[:, scblk * 16:(scblk + 1) * 16],
                                                        scalar1=1.0, scalar2=None, op0=ALU.add),
                 reads=['modF'], writes=['amod'])
            ngv = ng[:, which * 8:(which + 1) * 8].unsqueeze(2).to_broadcast([128, 8, 2])
            av = amod[:, which * 16:(which + 1) * 16].rearrange("p (k b) -> p k b", b=2)
            T.op('dve', lambda: nc.vector.tensor_tensor(out=av, in0=av, in1=ngv, op=ALU.mult),
                 reads=['amod', 'ng'], writes=['amod'])
        T.barrier()

        store_toks = []
        for part in range(2):
            g = GEOMS[part]
            midx, mn, many, mall = MASKS[part]
            b = part
            NT = g.ntile
            TC = NT * 128
            NS = g.nslot
            SC = NS * 128
            hT = abf(0, [8, 2816])
            win = abf(44, [8, 2048])
            wpool = abf(76, [4, 128])
            uv = abf(77, [22, 512])
            catT = abf(99, [8, 2176])
            wout = abf(169.5, [8, 1024])
            vaug = abf(169.5, [22, 2, 128])
            xt = [af32(133 + 4 * i, [1024]) for i in range(3)]
            xn = [abf(145 + 2 * i, [1024]) for i in range(4)]
            hmul = [af32(155 + 4 * i, [8, 128]) for i in range(4)]
            junk = [abf(153, [1024]), abf(171, [1024])]
            mpool = abf(155, [60, 128])
            mixT = [abf(173 + i, [4, 128]) for i in range(2)]
            qc = abf(133, [2816])
            kc_ = abf(138.5, [2816])
            T2 = af32(144, [2, 1024])
            T3m = af32(152, [2, 10, 64])
            PT = [abf(157 + 0.5 * i, [256]) for i in range(6)]
            rd = [af32(161.5 + i, [256]) for i in range(2)]
            sq = [abf(163.5 + i, [512]) for i in range(2)]
            rs = [af32(165.5 + 2 * i, [512]) for i in range(2)]
            x1 = af32(0, [17, 1024])
            otmp = [af32(133 + 4 * i, [1024]) for i in range(2)]

            a1 = amod[:, 0:16].rearrange("p (k b) -> p k b", b=2)
            sh1 = modF[:, 0:16].rearrange("p (k b) -> p k b", b=2)
            a2 = amod[:, 16:32].rearrange("p (k b) -> p k b", b=2)
            sh2 = modF[:, 32:48].rearrange("p (k b) -> p k b", b=2)

            for kc in range(8):
                T.dma('pool', 'win%d' % kc,
                      [lambda kc=kc: nc.gpsimd.dma_start(out=win[:, kc, :], in_=win_v[:, kc, :])],
                      writes=['win%d' % kc])
            T.dma('pool', 'wpool', [lambda: nc.gpsimd.dma_start(out=wpool, in_=wpool_v)], writes=['wpool'])
            WIN_ALL = ['win%d' % kc for kc in range(8)]
            T.dma('sp', 'mpool', [lambda: nc.sync.dma_start(out=mpool, in_=mpool_d)], writes=['mpool'])

            def norm_tile(src_ap, src_res, n, amodv, shv, dst_fn, dst_res, sidx):
                pass

            def rms_stage_a(src_ap, src_res, i, sidx):
                jb = junk[i % 2]
                sc = stat[:, sidx * 2:sidx * 2 + 1]
                sr = stat[:, sidx * 2 + 1:sidx * 2 + 2]
                sres = 'stat%d' % sidx
                T.op('act', lambda: nc.scalar.activation(out=jb, in_=src_ap, func=AF.Square, accum_out=sc),
                     reads=[src_res], writes=['junk%d' % (i % 2), sres])
                T.op('act', lambda: nc.scalar.activation(out=sr, in_=sc, func=AF.Sqrt, scale=1.0 / D, bias=epsb[:, 0:1]),
                     reads=[sres, 'epsb'], writes=[sres + 'r'])
                T.op('dve', lambda: nc.vector.reciprocal(out=sr, in_=sr), reads=[sres + 'r'], writes=[sres + 'r'])

            def rms_stage_b(src_ap, src_res, i, sidx):
                xnb = xn[i % len(xn)]
                xnres = 'xn%d' % (i % len(xn))
                sr = stat[:, sidx * 2 + 1:sidx * 2 + 2]
                sres = 'stat%d' % sidx
                T.op('act', lambda: nc.scalar.activation(out=xnb, in_=src_ap, func=AF.Copy, scale=sr),
                     reads=[src_res, sres + 'r'], writes=[xnres])
                bk = i % 2
                pT = ps[:, bk * 512:(bk + 1) * 512].bitcast(BF16).rearrange("p (k t) -> p k t", k=8)

                def tr():
                    for kc in range(8):
                        ins = nc.tensor.transpose(pT[:, kc, :], xnb[:, kc * 128:(kc + 1) * 128], ident[:])
                    return ins
                T.op('pe', tr, reads=[xnres, 'ident'], writes=['bank%d' % bk])
                return pT, 'bank%d' % bk

            def recip(out_ap, in_ap, scratch_ap, reads, writes, sres):
                T.op('dve', lambda: nc.vector.reciprocal_approx_accurate(out=out_ap, in_=in_ap, scratch=scratch_ap),
                     reads=reads, writes=list(writes) + [sres])

            epsb = sb("epsb%d" % part, [128, 2], F32)
            T.op('dve', lambda: nc.vector.memset(epsb[:, 0:1], EPS), writes=['epsb'])
            T.op('dve', lambda: nc.vector.memset(epsb[:, 1:2], EPS), writes=['epsb'])

            def a1_tail(t):
                pT, bres = rms_stage_b(xt[t % 3], 'xt%d' % (t % 3), t, t % 8)
                def mod1():
                    for kc in range(8):
                        ins = nc.vector.scalar_tensor_tensor(
                            out=hT[:, kc, t * 128:(t + 1) * 128], in0=pT[:, kc, :], scalar=a1[:, kc, b:b + 1],
                            in1=sh1[:, kc, b:b + 1].to_broadcast([128, 128]), op0=ALU.mult, op1=ALU.add)
                    return ins
                T.op('dve', mod1, reads=[bres, 'amod', 'modF'], writes=['hT%d' % t])

            def inproj_tok(t, col0, dst_res, evac='act'):
                bk = 2 + (t % 2)

                def mm():
                    for kc in range(8):
                        ins = nc.tensor.matmul(bank(bk), lhsT=hT[:, kc, t * 128:(t + 1) * 128],
                                               rhs=win[:, kc, col0:col0 + 512], start=(kc == 0), stop=(kc == 7))
                    return ins
                T.op('pe', mm, reads=['hT%d' % t] + WIN_ALL, writes=['bank%d' % bk])
                if evac == 'act':
                    T.op('act', lambda: nc.scalar.copy(out=uv[:, t, :], in_=bank(bk)), reads=['bank%d' % bk],
                         writes=[dst_res])
                else:
                    T.op('dve', lambda: nc.vector.tensor_copy(out=uv[:, t, :], in_=bank(bk)), reads=['bank%d' % bk],
                         writes=[dst_res])

            for t in range(NT + 1):
                if t < NT:
                    xb = xt[t % 3]
                    xres = 'xt%d' % (t % 3)
                    T.dma('sp', xres, [lambda: nc.sync.dma_start(out=xb, in_=xin[part][t * 128:(t + 1) * 128, :])],
                          writes=[xres])
                    rms_stage_a(xb, xres, t, t % 8)
                if t >= 1:
                    a1_tail(t - 1)
                if t >= 3:
                    inproj_tok(t - 3, 1536, 'uv%d' % (t - 3), evac='dve')
            for t in range(max(NT - 2, 0), NT):
                inproj_tok(t, 1536, 'uv%d' % t, evac='dve')


            if part == 0:
                ptiles = [(t, 0 if t == 0 else (2 if t == 15 else 1), t, 0, 128, 0) for t in range(16)]
            else:
                ptiles = [(t, 3 if t == 3 else (4 if t == 18 else 1), t - 3, 0, 128, 0) for t in range(3, 19)]
                ptiles.append((2, 1, 16, 64, 64, 0))
                ptiles.append((19, 1, 16, 0, 64, 64))
            ptiles.sort(key=lambda p_: p_[0])
            vdone = 0
            for pi, (t, var, slot, c0, ncol, sc0) in enumerate(ptiles):
                pp = bank(4).rearrange("p (g t) -> p g t", g=4)
                pw = bank(5).rearrange("p (g t) -> p g t", g=4)
                nb = [j for j in range(3) if 0 <= t + j - 1 < NT]

                def mm_pool():
                    for gi in range(4):
                        for n_, j in enumerate(nb):
                            ins = nc.tensor.matmul(pp[:, gi, :], lhsT=uv[:, t + j - 1, gi * 128:(gi + 1) * 128],
                                                   rhs=mpool[:, (var * 3 + j) * 4 + gi, :],
                                                   start=(n_ == 0), stop=(n_ == len(nb) - 1))
                    return ins
                T.op('pe', mm_pool, reads=['uv%d' % (t + j - 1) for j in nb] + ['mpool'], writes=['bank4'])
                mx = mixT[pi % 2]
                mres = 'mixT%d' % (pi % 2)
                T.op('dve', lambda: nc.vector.tensor_copy(out=mx, in_=pp), reads=['bank4'], writes=[mres])
                while vdone <= t - 1 and vdone < g.nkv:
                    inproj_tok(vdone, 1024, 'uv%d' % vdone)
                    vdone += 1

                def mm_wp():
                    for gi in range(4):
                        ins = nc.tensor.matmul(pw[:, gi, :], lhsT=wpool[:, gi, :], rhs=mx[:, gi, :], start=True, stop=True)
                    return ins
                T.op('pe', mm_wp, reads=[mres, 'wpool'], writes=['bank5'])
                psb = pscale[:, :].unsqueeze(2).to_broadcast([128, 4, ncol])
                T.op('dve', lambda: nc.vector.tensor_tensor(
                    out=catT[:, 4:8, slot * 128 + sc0:slot * 128 + sc0 + ncol], in0=pw[:, :, c0:c0 + ncol], in1=psb,
                    op=ALU.mult), reads=['bank5', 'pscale'], writes=['catP%d_%d' % (slot, sc0)])

            for t in range(vdone, g.nkv):
                inproj_tok(t, 1024, 'uv%d' % t)

            T.barrier()
            def qk_stages(c_, which, gi, seqn=0):
                dst, dres = (qc, 'qc') if which == 0 else (kc_, 'kc')
                bA = [0, 1, 6, 7][seqn % 4]
                bB = 4 + seqn % 2
                t0 = gi * 4
                nt = min(4, NT - t0)
                need = [t for t in range(t0, t0 + nt)
                        if ((t in g.qtiles) if which == 0 else (t < g.nkv))]
                t0, nt = need[0], need[-1] - need[0] + 1
                N = nt * 128
                cols = slice(t0 * 128, t0 * 128 + N)
                wcol = which * 512 + c_ * 128
                sqb = sq[seqn % 2]
                sres = 'sq%d' % (seqn % 2)
                rsb = rs[seqn % 2]
                rres = 'rs%d' % (seqn % 2)

                def st0():
                    def mm_qk():
                        for kc in range(8):
                            ins = nc.tensor.matmul(bank(bA)[:, 0:N], lhsT=win[:, kc, wcol:wcol + 128],
                                                   rhs=hT[:, kc, cols], start=(kc == 0), stop=(kc == 7))
                        return ins
                    T.op('pe', mm_qk, reads=['hT%d' % t for t in range(t0, t0 + nt)] + WIN_ALL, writes=['bank%d' % bA])

                def st1():
                    T.op('act', lambda: nc.scalar.activation(out=sqb[:, 0:N], in_=bank(bA)[:, 0:N], func=AF.Square),
                         reads=['bank%d' % bA], writes=[sres])

                def st2():
                    T.op('pe', lambda: nc.tensor.matmul(bank(bB)[:, 0:N], lhsT=bones[:], rhs=sqb[:, 0:N],
                                                        start=True, stop=True),
                         reads=[sres, 'bones'], writes=['bank%d' % bB])

                def st3():
                    T.op('act', lambda: nc.scalar.activation(out=rsb[:, 0:N], in_=bank(bB)[:, 0:N], func=AF.Ln,
                                                             scale=1.0 / HD, bias=epsb[:, 1:2]),
                         reads=['bank%d' % bB, 'epsb'], writes=[rres])
                    T.op('act', lambda: nc.scalar.activation(out=rsb[:, 0:N], in_=rsb[:, 0:N], func=AF.Exp, scale=-0.5),
                         reads=[rres], writes=[rres])

                def st4():
                    T.op('dve', lambda: nc.vector.scalar_tensor_tensor(
                        out=dst[:, cols], in0=bank(bA)[:, 0:N], scalar=qkg[:, which:which + 1], in1=rsb[:, 0:N],
                        op0=ALU.mult, op1=ALU.mult),
                        reads=['bank%d' % bA, 'qkg', rres], writes=['%s%d' % (dres, gi)])
                return [st0, st1, st2, st3, st4]

            def qk_group(c_, which, gi):
                for st in qk_stages(c_, which, gi):
                    st()

            ngrp = (NT + 3) // 4
            T.op('pool', lambda: nc.gpsimd.memset(vaug[:, :, :, 64:128], 1.0), writes=['vaug'])
            for c in range(4):
                T.dma('sp', 'T2', [lambda: nc.sync.dma_start(out=T2, in_=rpbG_d[2 * c:2 * c + 2].rearrange("h p f -> p h f"))],
                      writes=['T2', 'mpool', 'mixT0', 'mixT1'])
                T2v = T2.rearrange("p h (s q) -> p (h s) q", q=64)
                cmb = cmask[:, :].unsqueeze(1).to_broadcast([128, 32, 64])
                T.op('dve', lambda: nc.vector.tensor_tensor(out=T2v, in0=T2v, in1=cmb, op=ALU.add),
                     reads=['T2', 'cmask'], writes=['T2'])
                T3S = [(0, 0), (0, 1), (1, 0), (1, 1), (1, 2), (1, 3), (4, 1), (4, 2), (4, 3), (5, 3)]
                for k_, (o, i) in enumerate(T3S):
                    sp_ = 11 - 2 * o
                    mi = midx[(1, o, i)]
                    T.op('dve', lambda: nc.vector.tensor_scalar(
                        out=T3m[:, :, k_, :], in0=T2[:, :, (sp_ + i) * 64:(sp_ + i + 1) * 64],
                        scalar1=rmask[part][:, mi:mi + 1], scalar2=None, op0=ALU.add),
                        reads=['T2', 'rmask%d' % part], writes=['T3m'])
                for hh in range(2):
                    T.op('act', lambda: nc.scalar.copy(
                        out=vaug[:, 0:g.nkv, hh, 0:64], in_=uv[:, 0:g.nkv, (2 * c + hh) * 64:(2 * c + hh + 1) * 64]),
                        reads=['uv%d' % t for t in range(g.nkv)], writes=['vaug'])
                qgl = [qk_stages(c, which, gi, n_) for n_, (which, gi) in
                       enumerate([(w_, g_) for g_ in range(ngrp) for w_ in range(2)])]
                for step in range(len(qgl) + 4):
                    for k_ in reversed(range(5)):
                        idx = step - k_
                        if 0 <= idx < len(qgl):
                            qgl[idx][k_]()
                items = []
                border = list(range(len(g.blocks)))
                if part == 1:
                    border = [8] + list(range(8)) + [9]
                for bi in border:
                    blk = g.blocks[bi]
                    nrow = blk['nrow']
                    order = sorted(blk['pairs'], key=lambda p: (abs(p[2] - 2.4)))
                    plist = []
                    for (kt, sp, o) in order:
                        rows = [i for i in range(nrow) if many[(bi, o, i)]]
                        if not rows:
                            continue
                        plist.append((kt, sp, o, min(rows), max(rows) + 1))
                    assert plist[0][3] == 0 and plist[0][4] == nrow
                    for hh in range(2):
                        items.append(dict(bi=bi, blk=blk, hh=hh, nq=nrow * 64, plist=plist,
                                          units=[[p_] for p_ in plist]))
                ulist = []
                for it_i, it in enumerate(items):
                    for ui, unit in enumerate(it['units']):
                        ulist.append((it_i, ui, unit))
                SBANK = [[0], [1], [4], [5], [6], [7]]
                NSB = 6

                def emit_S(u):
                    it_i, ui, unit = ulist[u]
                    it = items[it_i]
                    blk, hh, bi = it['blk'], it['hh'], it['bi']
                    sbi = u % NSB
                    bks = SBANK[sbi]
                    interior = (blk['nrow'] == 4 and 1 <= bi <= 6)
                    qgs = sorted(set((blk['qcol0'] + i * 64) // 512 for i in range(blk['nrow'])))
                    kgs = sorted(set(kt // 4 for (kt, _, _, _, _) in unit))

                    def mm_s():
                        for j, (kt, sp, o, i0, i1) in enumerate(unit):
                            ins = nc.tensor.matmul(
                                bank(bks[j])[:, i0 * 64:i1 * 64],
                                lhsT=kc_[hh * 64:(hh + 1) * 64, kt * 128:(kt + 1) * 128],
                                rhs=qc[hh * 64:(hh + 1) * 64, blk['qcol0'] + i0 * 64:blk['qcol0'] + i1 * 64],
                                start=True, stop=True)
                        return ins
                    T.op('pe', mm_s, reads=['kc%d' % x for x in kgs] + ['qc%d' % x for x in qgs],
                         writes=['bank%d' % bks[j] for j in range(len(unit))] + ['S%d_%d' % (sbi, j) for j in range(len(unit))])
                    for j, (kt, sp, o, i0, i1) in enumerate(unit):
                        bkn = 'bank%d' % bks[j]
                        Sj = bank(bks[j])
                        Sv = Sj[:, i0 * 64:i1 * 64]
                        allok = all(mall[(bi, o, i)] for i in range(i0, i1))
                        if interior and (o, i0) in T3S:
                            k0 = T3S.index((o, i0))
                            assert all(T3S[k0 + d] == (o, i0 + d) for d in range(i1 - i0))
                            btab = T3m[:, hh, k0:k0 + (i1 - i0), :]
                            T.op('dve', lambda: nc.vector.tensor_tensor(
                                out=Sv.rearrange("p (a b) -> p a b", b=64), in0=Sv.rearrange("p (a b) -> p a b", b=64),
                                in1=btab, op=ALU.add),
                                reads=[bkn, 'T3m'], writes=['S%d_%d' % (sbi, j)])
                        elif allok:
                            T.op('dve', lambda: nc.vector.tensor_tensor(
                                out=Sv, in0=Sv, in1=T2[:, hh, (sp + i0) * 64:(sp + i1) * 64], op=ALU.add),
                                reads=[bkn, 'T2'], writes=['S%d_%d' % (sbi, j)])
                        else:
                            def bias_rows():
                                for i in range(i0, i1):
                                    mi = midx[(bi, o, i)]
                                    ins = nc.vector.scalar_tensor_tensor(
                                        out=Sj[:, i * 64:(i + 1) * 64],
                                        in0=Sj[:, i * 64:(i + 1) * 64],
                                        scalar=rmask[part][:, mi:mi + 1],
                                        in1=T2[:, hh, (sp + i) * 64:(sp + i + 1) * 64],
                                        op0=ALU.add, op1=ALU.add)
                                return ins
                            T.op('dve', bias_rows, reads=[bkn, 'T2', 'rmask%d' % part], writes=['S%d_%d' % (sbi, j)])
                        T.op('act', lambda: nc.scalar.activation(
                            out=PT[sbi][:, j * 256 + i0 * 64:j * 256 + i1 * 64], in_=Sv, func=AF.Exp),
                            reads=[bkn, 'S%d_%d' % (sbi, j)], writes=['PT%d_%d' % (sbi, j)])

                def emit_PV(u):
                    it_i, ui, unit = ulist[u]
                    it = items[it_i]
                    blk, hh, bi, nq = it['blk'], it['hh'], it['bi'], it['nq']
                    h = 2 * c + hh
                    sbi = u % NSB
                    ob = it_i % 2
                    O = bank(2 + ob)
                    ores = 'bank%d' % (2 + ob)
                    nun = len(it['units'])

                    def mm_pv():
                        for j, (kt, sp, o, i0, i1) in enumerate(unit):
                            first = (ui == 0 and j == 0)
                            last = (ui == nun - 1 and j == len(unit) - 1)
                            ins = nc.tensor.matmul(O[:, i0 * 64:i1 * 64], lhsT=vaug[:, kt, hh, :],
                                                   rhs=PT[sbi][:, j * 256 + i0 * 64:j * 256 + i1 * 64],
                                                   start=first, stop=last)
                        return ins
                    T.op('pe', mm_pv,
                         reads=['PT%d_%d' % (sbi, j) for j in range(len(unit))] + ['vaug'],
                         writes=[ores])
                    if ui == nun - 1:
                        rdb = rd[ob]
                        rdres = 'rd%d' % ob
                        T.op('act', lambda: nc.scalar.activation(out=rdb[0:64, 0:nq], in_=O[64:128, 0:nq], func=AF.Ln),
                             reads=[ores], writes=[rdres])
                        T.op('act', lambda: nc.scalar.activation(out=rdb[0:64, 0:nq], in_=rdb[0:64, 0:nq], func=AF.Exp,
                                                                 scale=-1.0), reads=[rdres], writes=[rdres])
                        T.op('dve', lambda: nc.vector.tensor_tensor(
                            out=catT[hh * 64:(hh + 1) * 64, c, blk['scol0']:blk['scol0'] + nq],
                            in0=O[0:64, 0:nq], in1=rdb[0:64, 0:nq], op=ALU.mult),
                            reads=[ores, rdres], writes=['catA%d_%d_%d' % (c, hh, bi)])

                LA = 5
                U = len(ulist)
                pend = []
                if False:
                    lastq = {}
                    lastk = {}
                    for u_, (it_i, ui, unit) in enumerate(ulist):
                        blk_ = items[it_i]['blk']
                        for i in range(blk_['nrow']):
                            lastq[(blk_['qcol0'] + i * 64) // 512] = u_
                        for (kt, _, _, _, _) in unit:
                            lastk[kt // 4] = u_
                    for gi in range(ngrp):
                        pend.append((lastq.get(gi, -1) + 1, 0, gi))
                        pend.append((lastk.get(gi, -1) + 1, 1, gi))
                    pend.sort()
                active = []
                nxt = 4
                for u in range(U + LA):
                    if u < U:
                        emit_S(u)
                    if u - LA >= 0:
                        emit_PV(u - LA)
                    if active:
                        if u >= nxt:
                            active.pop(0)()
                            nxt = u + 2
                    elif pend and u >= nxt and pend[0][0] <= u - 2:
                        _, which_, gi_ = pend.pop(0)
                        active = qk_stages(c + 1, which_, gi_)
                        active.pop(0)()
                        nxt = u + 2
                for st in active:
                    st()
                for (_, which_, gi_) in pend:
                    qk_group(c + 1, which_, gi_)

            T.barrier()
            T.dma('pool', 'wout', [(lambda kc=kc: nc.gpsimd.dma_start(out=wout[:, kc, :], in_=wout_v[:, kc, :]))
                                   for kc in range(8)], writes=['wout', 'vaug'])
            CAT_ALL = None
            for s in range(NS):
                t = g.qtiles[s]
                xres = 'x1_%d' % s
                T.dma('sp', xres, [lambda: nc.sync.dma_start(out=x1[:, s, :], in_=xin[part][t * 128:(t + 1) * 128, :])],
                      writes=[xres])
                bk = 2 * (s % 2)

                def mm_o():
                    for half in range(2):
                        for kc in range(8):
                            ins = nc.tensor.matmul(bank(bk + half), lhsT=catT[:, kc, s * 128:(s + 1) * 128],
                                                   rhs=wout[:, kc, half * 512:(half + 1) * 512],
                                                   start=(kc == 0), stop=(kc == 7))
                    return ins
                T.op('pe', mm_o, reads=['wout'], writes=['bank%d' % bk, 'bank%d' % (bk + 1)])
                ot = otmp[s % 2]
                otres = 'otmp%d' % (s % 2)
                T.op('dve', lambda: nc.vector.tensor_tensor(out=ot, in0=ps[:, bk * 512:bk * 512 + 1024],
                                                            in1=gate[:, b * 2 + 0, :], op=ALU.mult),
                     reads=['bank%d' % bk, 'bank%d' % (bk + 1), 'gate'], writes=[otres])
                T.op('pool', lambda: nc.gpsimd.tensor_tensor(out=x1[:, s, :], in0=x1[:, s, :], in1=ot, op=ALU.add),
                     reads=[otres, xres], writes=[xres])
            T.barrier()

            h2T = abf(68, [8, 2050])
            WU0 = 103.0
            WD0 = 147.0
            wup_u = [abf(WU0 + 4 * j, [8, 256]) for j in range(11)]
            wdn_u = [abf(WD0 + 2 * j, [1024]) for j in range(11)]
            G = abf(169, [13, 256])
            cg = [af32(175.5 + i, [256]) for i in range(3)]
            cv = [af32(178.5 + i, [256]) for i in range(3)]
            sg = [af32(181.5 + i, [256]) for i in range(3)]
            xn = [abf(169 + 2 * i, [1024]) for i in range(3)]
            junk = [abf(183, [1024]), abf(100.5, [1024])]
            dtmp_b = [af32(100.5, [512]), af32(184.5, [512])]
            hmul = [af32(175 + 4 * i, [8, 128]) for i in range(2)]

            def load_up(half, j):
                jj = half * 11 + j
                T.dma('pool', 'wupu%d' % j, [
                    lambda: nc.gpsimd.dma_start(out=wup_u[j][:, :, 0:128], in_=wup_v[:, :, jj * 128:(jj + 1) * 128]),
                    lambda: nc.gpsimd.dma_start(out=wup_u[j][:, :, 128:256],
                                                in_=wup_v[:, :, FF + jj * 128:FF + (jj + 1) * 128])],
                    writes=['wupu%d' % j])

            def load_dn(half, j):
                jj = half * 11 + j
                T.dma('pool', 'wdnu%d' % j, [lambda: nc.gpsimd.dma_start(out=wdn_u[j], in_=wdn_v[:, jj, :])],
                      writes=['wdnu%d' % j])

            for j in range(11):
                load_up(0, j)
            for j in range(11):
                load_dn(0, j)
            if part == 0:
                T.op('dve', lambda: nc.vector.memset(h2T[:, :, 0:1], 0.0), writes=['h2halo0'])
                T.op('dve', lambda: nc.vector.memset(h2T[:, :, 2049:2050], 0.0), writes=['h2halo1'])
            def n2_tail(s):
                pT, bres = rms_stage_b(x1[:, s, :], 'x1_%d' % s, s, 8 + s % 8)
                hm = hmul[s % len(hmul)]
                hres = 'hmul%d' % (s % len(hmul))
                a2b = a2[:, :, b:b + 1].to_broadcast([128, 8, 128])
                s2b = sh2[:, :, b:b + 1].to_broadcast([128, 8, 128])
                if s < 16:
                    def mod2():
                        for kc in range(8):
                            ins = nc.vector.scalar_tensor_tensor(
                                out=h2T[:, kc, 1 + s * 128:1 + (s + 1) * 128], in0=pT[:, kc, :],
                                scalar=a2[:, kc, b:b + 1], in1=sh2[:, kc, b:b + 1].to_broadcast([128, 128]),
                                op0=ALU.mult, op1=ALU.add)
                        return ins
                    T.op('dve', mod2, reads=[bres, 'amod', 'modF'], writes=['h2T%d' % s])
                else:
                    T.op('dve', lambda: nc.vector.tensor_tensor(out=hm, in0=pT, in1=a2b, op=ALU.mult),
                         reads=[bres, 'amod'], writes=[hres])
                    s2c = sh2[:, :, b:b + 1]
                    T.op('pool', lambda: nc.gpsimd.tensor_tensor(out=hm[:, :, 63:65], in0=hm[:, :, 63:65],
                                                                 in1=s2c.to_broadcast([128, 8, 2]), op=ALU.add),
                         reads=[hres, 'modF'], writes=[hres])
                    T.op('pool', lambda: nc.gpsimd.tensor_scalar(out=h2T[:, :, 0:1], in0=hm[:, :, 63:64],
                                                                 scalar1=halov[:, 0:1], scalar2=None, op0=ALU.mult),
                         reads=[hres, 'halov'], writes=['h2halo0'])
                    T.op('pool', lambda: nc.gpsimd.tensor_scalar(out=h2T[:, :, 2049:2050], in0=hm[:, :, 64:65],
                                                                 scalar1=halov[:, 1:2], scalar2=None, op0=ALU.mult),
                         reads=[hres, 'halov'], writes=['h2halo1'])

            for s in range(NS + 1):
                if s < NS:
                    rms_stage_a(x1[:, s, :], 'x1_%d' % s, s, 8 + s % 8)
                if s >= 1:
                    n2_tail(s - 1)
            T.barrier()

            cw = convw[:, :].rearrange("p (c k) -> p c k", k=3)
            NG = 8
            seq = [(half, grp, j) for half in range(2) for grp in range(NG) for j in range(11)]
            nseq = len(seq)

            def gslot(half, grp, j):
                return (11 * (half * NG + grp) + j) % 13

            def hreads_of(grp):
                r = ['h2T%d' % (2 * grp), 'h2T%d' % (2 * grp + 1), 'h2halo0', 'h2halo1']
                if grp > 0:
                    r.append('h2T%d' % (2 * grp - 1))
                if grp < NG - 1:
                    r.append('h2T%d' % (2 * grp + 2))
                return r

            def emit_up(i):
                half, grp, j = seq[i]
                st_ = i % 3
                pg, pv = bank(2 * st_), bank(2 * st_ + 1)
                g0 = grp * 256

                def mm_up():
                    for which, pbk in ((0, pg), (1, pv)):
                        for kc in range(8):
                            ins = nc.tensor.matmul(pbk[:, 0:258], lhsT=wup_u[j][:, kc, which * 128:(which + 1) * 128],
                                                   rhs=h2T[:, kc, g0:g0 + 258], start=(kc == 0), stop=(kc == 7))
                    return ins
                T.op('pe', mm_up, reads=hreads_of(grp) + ['wupu%d' % j],
                     writes=['bank%d' % (2 * st_), 'bank%d' % (2 * st_ + 1)])
                if half == 0 and grp == NG - 1:
                    load_up(1, j)
                jj = half * 11 + j
                chg, chv = jj, 22 + jj
                T.op('act', lambda: nc.scalar.activation(out=cg[st_], in_=pg[:, 1:257], func=AF.Identity,
                                                         scale=cw[:, chg, 1:2], bias=convb[:, chg:chg + 1]),
                     reads=['bank%d' % (2 * st_), 'convw', 'convb'], writes=['cg%d' % st_])
                T.op('act', lambda: nc.scalar.activation(out=cv[st_], in_=pv[:, 1:257], func=AF.Identity,
                                                         scale=cw[:, chv, 1:2], bias=convb[:, chv:chv + 1]),
                     reads=['bank%d' % (2 * st_ + 1), 'convw', 'convb'], writes=['cv%d' % st_])

            def emit_taps(i):
                half, grp, j = seq[i]
                st_ = i % 3
                pg, pv = bank(2 * st_), bank(2 * st_ + 1)
                jj = half * 11 + j
                chg, chv = jj, 22 + jj
                for (buf, pbk, ch, nm, bk_) in ((cg[st_], pg, chg, 'cg%d' % st_, 2 * st_),
                                                (cv[st_], pv, chv, 'cv%d' % st_, 2 * st_ + 1)):
                    T.op('dve', lambda: nc.vector.scalar_tensor_tensor(
                        out=buf, in0=pbk[:, 0:256], scalar=cw[:, ch, 0:1], in1=buf, op0=ALU.mult, op1=ALU.add),
                        reads=['bank%d' % bk_, nm, 'convw'], writes=[nm])
                    T.op('dve', lambda: nc.vector.scalar_tensor_tensor(
                        out=buf, in0=pbk[:, 2:258], scalar=cw[:, ch, 2:3], in1=buf, op0=ALU.mult, op1=ALU.add),
                        reads=['bank%d' % bk_, nm, 'convw'], writes=[nm])
                T.op('act', lambda: nc.scalar.activation(out=sg[st_], in_=cg[st_], func=AF.Silu),
                     reads=['cg%d' % st_], writes=['sg%d' % st_])
                gs_ = gslot(half, grp, j)
                T.op('pool', lambda: nc.gpsimd.tensor_tensor(out=G[:, gs_, :], in0=cv[st_], in1=sg[st_], op=ALU.mult),
                     reads=['cv%d' % st_, 'sg%d' % st_], writes=['G%d' % gs_])

            def emit_down(half, grp):
                for tt in range(2):
                    s_ = 2 * grp + tt
                    xres = 'x1_%d' % s_
                    for h2 in range(2):
                        def mm_dn():
                            for j in range(11):
                                ins = nc.tensor.matmul(bank(6 + h2), lhsT=G[:, gslot(half, grp, j), tt * 128:(tt + 1) * 128],
                                                       rhs=wdn_u[j][:, h2 * 512:(h2 + 1) * 512],
                                                       start=(j == 0), stop=(j == 10))
                            return ins
                        T.op('pe', mm_dn, reads=['G%d' % gslot(half, grp, j) for j in range(11)] + ['wdnu%d' % j for j in range(11)],
                             writes=['bank%d' % (6 + h2)])
                        dt_ = dtmp_b[h2]
                        dres = 'dtmpb%d' % h2
                        T.op('dve', lambda: nc.vector.tensor_tensor(
                            out=dt_, in0=bank(6 + h2), in1=gate[:, b * 2 + 1, h2 * 512:(h2 + 1) * 512], op=ALU.mult),
                            reads=['bank%d' % (6 + h2), 'gate'], writes=[dres])
                        T.op('pool', lambda: nc.gpsimd.tensor_tensor(
                            out=x1[:, s_, h2 * 512:(h2 + 1) * 512], in0=x1[:, s_, h2 * 512:(h2 + 1) * 512], in1=dt_,
                            op=ALU.add), reads=[dres, xres], writes=[xres])
                    if half == 1:
                        tok = T.dma('sp', 'st%d' % s_,
                                    [lambda: nc.sync.dma_start(out=yout[part][s_ * 128:(s_ + 1) * 128, :], in_=x1[:, s_, :])],
                                    reads=[xres])
                        store_toks.append(tok)
                if half == 0 and grp == NG - 1:
                    for j in range(11):
                        load_dn(1, j)

            for i in range(nseq + 2):
                if i < nseq:
                    emit_up(i)
                k = i - 1
                if 0 <= k < nseq:
                    emit_taps(k)
                if 0 <= k < nseq and seq[k][2] == 1 and k >= 12:
                    ph, pg_, _ = seq[k - 12]
                    emit_down(ph, pg_)
            emit_down(1, NG - 1)
            T.barrier()

        for tok in store_toks:
            T._wait('sp', tok)
        T.barrier()
    return nc


_NC_CACHE = {}


def kernel(x_prompt, x_sample, c_prompt, c_sample, w_ada, b_ada, norm1_g, norm2_g,
           w_in, q_norm_g, k_norm_g, rpb, w_pool, pool_scale, w_out, w_up,
           conv_w, conv_b, w_down):
    f32 = np.float32
    A = lambda a: np.ascontiguousarray(np.asarray(a, dtype=f32))
    x_prompt, x_sample = A(x_prompt), A(x_sample)
    st = static_tables()
    dr, dc = rpb_gather_index()
    rpbG = A(rpb)[0][:, dr, dc].reshape(NH, 128, 1024)
    fm = lambda v: np.ascontiguousarray(A(v).reshape(-1, 128).T)
    ngv = np.concatenate([fm(A(norm1_g)[0]), fm(A(norm2_g)[0])], axis=1)
    qkg = np.stack([np.tile(A(q_norm_g)[0], 2), np.tile(A(k_norm_g)[0], 2)], axis=1)
    pscale = fm(A(pool_scale)[0])
    cw = A(conv_w)[0]
    convw = np.ascontiguousarray(cw.reshape(3, 44, 128).transpose(2, 1, 0)).reshape(128, 132)
    convb = fm(A(conv_b)[0])
    shared = dict(
        w_ada=A(w_ada)[0], b_ada2=np.ascontiguousarray(np.broadcast_to(A(b_ada)[0][None, :], (2, 6 * D))),
        ng=ngv, w_in=A(w_in)[0], qkg=np.ascontiguousarray(qkg), rpbG=np.ascontiguousarray(rpbG),
        cmask=st['cmask'], w_pool=A(w_pool)[0], pscale=pscale, w_out=A(w_out)[0], w_up=A(w_up)[0],
        convw=convw, convb=convb, w_down=A(w_down)[0], ident=st['ident'], bones=st['bones'], sel=st['sel'],
        i2=st['i2'])
    in_maps = []
    for core in range(N_CORES):
        sb_, ch = core // 4, core % 4
        R0 = 32 * ch
        xs = np.zeros((22, 2, 64, D), f32)
        xseq = x_sample[sb_].reshape(128, 64, D)
        for t in range(21):
            for hh in range(2):
                r = R0 - 6 + 2 * t + hh
                if 0 <= r < 128:
                    xs[t, hh] = xseq[r]
        if R0 - 1 >= 0:
            xs[21, 0] = xseq[R0 - 1]
        if R0 + 32 < 128:
            xs[21, 1] = xseq[R0 + 32]
        cT = np.stack([fm(A(c_prompt)[core]), fm(A(c_sample)[sb_])], axis=2).reshape(128, 16)
        m = dict(shared)
        m.update(host_tables(core))
        m.update(xp=x_prompt[core], xs=xs.reshape(22 * 128, D), cT=np.ascontiguousarray(cT))
        in_maps.append(m)
    if 'nc' not in _NC_CACHE:
        _NC_CACHE['nc'] = build_nc()
    nc = _NC_CACHE['nc']
    res = run_bass_kernel_spmd(nc, in_maps, core_ids=list(range(N_CORES)))
    yp = np.stack([res.results[i]['yp'] for i in range(N_CORES)], axis=0).astype(f32)
    ys = np.stack([res.results[i]['ys'] for i in range(N_CORES)], axis=0).reshape(2, 4 * 2048, D).astype(f32)
    return yp, ys
```
